# Optimizing a Trainium2 kernel written in Bass

```python
import math
import jax
import jax.numpy as jnp
from jax import lax
import numpy as np

D_MODEL = 2048
BATCH = 4
SEQ = 4096
DEPTH = 2

HEAD_DIM = 128
D_FF = 5632
PLE_DIM = 256
ROPE_THETA = 10000.0
NORM_EPS = 1e-6
Q_BLOCK = 128
GATHER_Q_BLOCK = 64

A_HEADS = 8
A_QK_DIM = 64
A_V_DIM = 2 * A_QK_DIM
B_GROUPS = ((128, 1), (512, 4), (2048, 16))
B_HEADS_PER_GROUP = 4
B_HEADS = B_HEADS_PER_GROUP * len(B_GROUPS)
C_HEADS = 8
C_Q_LORA = 512
C_KV_LORA = 256
C_NOPE = 128
C_ROPE = 64
C_V = 128
D_HEADS = 8
MOBA_BLOCK = 256
MOBA_TOPK = 3

A_Q_WIDTH = A_HEADS * 2 * A_QK_DIM
A_V_WIDTH = A_HEADS * A_V_DIM
B_WIDTH = B_HEADS * HEAD_DIM
AB_SPLITS = (A_Q_WIDTH, A_Q_WIDTH, A_V_WIDTH, B_WIDTH, B_WIDTH, B_WIDTH)
AB_IN = sum(AB_SPLITS)
AB_OUT = A_V_WIDTH + B_HEADS_PER_GROUP * HEAD_DIM
D_WIDTH = D_HEADS * HEAD_DIM
CD_SPLITS = (C_Q_LORA, C_KV_LORA, C_ROPE, D_WIDTH, D_WIDTH, D_WIDTH)
CD_IN = sum(CD_SPLITS)
CD_OUT = C_HEADS * C_V + D_WIDTH

kernel_name = 'hybrid_diff_dilated_mla_moba_block'


def rms_norm(x, gain):
    xf = x.astype(jnp.float32)
    xf = xf * lax.rsqrt(jnp.mean(xf * xf, axis=-1, keepdims=True) + NORM_EPS)
    return (xf * gain.astype(jnp.float32)).astype(x.dtype)


def rope_tables(seq, dim):
    inv = 1.0 / (ROPE_THETA ** (jnp.arange(0, dim, 2, dtype=jnp.float32) / dim))
    ang = jnp.arange(seq, dtype=jnp.float32)[:, None] * inv[None, :]
    return jnp.cos(ang), jnp.sin(ang)


def apply_rope(x, cos, sin):
    x1, x2 = jnp.split(x, 2, axis=-1)
    c = cos.astype(x.dtype)
    s = sin.astype(x.dtype)
    return jnp.concatenate([x1 * c - x2 * s, x2 * c + x1 * s], axis=-1)


def split_cols(z, sizes):
    return jnp.split(z, np.cumsum(sizes)[:-1].tolist(), axis=-1)


def swiglu(h, w_gu, w_down):
    g, u = jnp.split(h @ w_gu, 2, axis=-1)
    return (jax.nn.silu(g) * u) @ w_down


def sweep_query_blocks(block_fn, seq, block):
    out = lax.map(block_fn, jnp.arange(seq // block))
    n, b, h, t, d = out.shape
    return jnp.moveaxis(out, 0, 2).reshape(b, h, n * t, d)


def causal_attention(q, k, v, scale):
    s = q.shape[2]
    kpos = jnp.arange(s)

    def block(i):
        q0 = i * Q_BLOCK
        qb = lax.dynamic_slice_in_dim(q, q0, Q_BLOCK, axis=2)
        qpos = q0 + jnp.arange(Q_BLOCK)
        sc = jnp.einsum('bhqd,bhkd->bhqk', qb, k).astype(jnp.float32) * scale
        sc = jnp.where(kpos[None, :] <= qpos[:, None], sc, -jnp.inf)
        pr = jax.nn.softmax(sc, axis=-1).astype(v.dtype)
        return jnp.einsum('bhqk,bhkd->bhqd', pr, v)

    return sweep_query_blocks(block, s, Q_BLOCK)


def diff_attention(q, k, v, lam, scale):
    s = v.shape[2]
    kpos = jnp.arange(s)

    def block(i):
        q0 = i * Q_BLOCK
        qb = lax.dynamic_slice_in_dim(q, q0, Q_BLOCK, axis=3)
        qpos = q0 + jnp.arange(Q_BLOCK)
        sc = jnp.einsum('bhmqd,bhmkd->bhmqk', qb, k).astype(jnp.float32) * scale
        sc = jnp.where(kpos[None, :] <= qpos[:, None], sc, -jnp.inf)
        pr = jax.nn.softmax(sc, axis=-1)
        attn = (pr[:, :, 0] - lam * pr[:, :, 1]).astype(v.dtype)
        return jnp.einsum('bhqk,bhkd->bhqd', attn, v)

    return sweep_query_blocks(block, s, Q_BLOCK)


def dilated_window_attention(q, k, v):
    b, _, s, d = q.shape
    n_g = len(B_GROUPS)
    qg_all = q.reshape(b, n_g, B_HEADS_PER_GROUP, s, d)
    kg_all = k.reshape(b, n_g, B_HEADS_PER_GROUP, s, d)
    vg_all = v.reshape(b, n_g, B_HEADS_PER_GROUP, s, d)
    scale = d ** -0.5

    def block(i):
        q0 = i * GATHER_Q_BLOCK
        qpos = q0 + jnp.arange(GATHER_Q_BLOCK)
        outs, lses = [], []
        for g, (window, dil) in enumerate(B_GROUPS):
            n_keys = window // dil + 1
            idx = qpos[:, None] - dil * jnp.arange(n_keys)[None, :]
            valid = idx >= 0
            idx = jnp.maximum(idx, 0)
            qb = lax.dynamic_slice_in_dim(qg_all[:, g], q0, GATHER_Q_BLOCK, axis=2)
            kb = jnp.take(kg_all[:, g], idx, axis=2)
            vb = jnp.take(vg_all[:, g], idx, axis=2)
            sc = jnp.einsum('bhqd,bhqnd->bhqn', qb, kb).astype(jnp.float32) * scale
            sc = jnp.where(valid, sc, -jnp.inf)
            lse = jax.nn.logsumexp(sc, axis=-1)
            pr = jnp.exp(sc - lse[..., None]).astype(v.dtype)
            outs.append(jnp.einsum('bhqn,bhqnd->bhqd', pr, vb))
            lses.append(lse)
        wts = jax.nn.softmax(jnp.stack(lses), axis=0).astype(v.dtype)
        return jnp.einsum('gbhq,gbhqd->bhqd', wts, jnp.stack(outs))

    return sweep_query_blocks(block, s, GATHER_Q_BLOCK)


def moba_attention(q, k, v):
    b, h, s, d = q.shape
    scale = d ** -0.5
    n_blk = -(-s // MOBA_BLOCK)
    pad = n_blk * MOBA_BLOCK - s
    widths = ((0, 0), (0, 0), (0, pad), (0, 0))
    k_pad = jnp.pad(k, widths)
    v_pad = jnp.pad(v, widths)
    k_blocks = k_pad.reshape(b, h, n_blk, MOBA_BLOCK, d)
    v_blocks = v_pad.reshape(b, h, n_blk, MOBA_BLOCK, d)
    k_mean = jnp.mean(k_blocks.astype(jnp.float32), axis=3).astype(k.dtype)
    top_k = min(MOBA_TOPK, n_blk - 1)
    n_sel = top_k * MOBA_BLOCK
    b_idx = jnp.arange(b)[:, None, None, None]
    h_idx = jnp.arange(h)[None, :, None, None]

    def block(i):
        q0 = i * GATHER_Q_BLOCK
        qpos = q0 + jnp.arange(GATHER_Q_BLOCK)
        own = q0 // MOBA_BLOCK
        qb = lax.dynamic_slice_in_dim(q, q0, GATHER_Q_BLOCK, axis=2)
        k_own = lax.dynamic_slice_in_dim(k_pad, own * MOBA_BLOCK, MOBA_BLOCK, axis=2)
        v_own = lax.dynamic_slice_in_dim(v_pad, own * MOBA_BLOCK, MOBA_BLOCK, axis=2)
        kpos = own * MOBA_BLOCK + jnp.arange(MOBA_BLOCK)
        s_own = jnp.einsum('bhqd,bhkd->bhqk', qb, k_own).astype(jnp.float32) * scale
        s_own = jnp.where(kpos[None, :] <= qpos[:, None], s_own, -jnp.inf)
        if top_k == 0:
            pr = jax.nn.softmax(s_own, axis=-1).astype(v.dtype)
            return jnp.einsum('bhqk,bhkd->bhqd', pr, v_own)
        gate = jnp.einsum('bhqd,bhnd->bhqn', qb, k_mean).astype(jnp.float32)
        gate = jnp.where(jnp.arange(n_blk) < own, gate, -jnp.inf)
        _, sel = lax.top_k(gate, top_k)
        sel_ok = sel < own
        k_sel = k_blocks[b_idx, h_idx, sel]
        v_sel = v_blocks[b_idx, h_idx, sel].reshape(b, h, GATHER_Q_BLOCK, n_sel, d)
        s_sel = jnp.einsum('bhqd,bhqnld->bhqnl', qb, k_sel).astype(jnp.float32) * scale
        s_sel = jnp.where(sel_ok[..., None], s_sel, -jnp.inf).reshape(b, h, GATHER_Q_BLOCK, n_sel)
        pr = jax.nn.softmax(jnp.concatenate([s_sel, s_own], axis=-1), axis=-1).astype(v.dtype)
        return (jnp.einsum('bhqn,bhqnd->bhqd', pr[..., :n_sel], v_sel)
                + jnp.einsum('bhqk,bhkd->bhqd', pr[..., n_sel:], v_own))

    return sweep_query_blocks(block, s, GATHER_Q_BLOCK)


def ab_mixer(h, w_in, lam_params, subln, w_out, layer_idx, rope64, rope128):
    b, s, _ = h.shape
    qa, ka, va, qb, kb, vb = split_cols(h @ w_in, AB_SPLITS)
    qa = apply_rope(qa.reshape(b, s, A_HEADS, 2, A_QK_DIM).transpose(0, 2, 3, 1, 4), *rope64)
    ka = apply_rope(ka.reshape(b, s, A_HEADS, 2, A_QK_DIM).transpose(0, 2, 3, 1, 4), *rope64)
    va = va.reshape(b, s, A_HEADS, A_V_DIM).transpose(0, 2, 1, 3)
    lam_init = 0.8 - 0.6 * math.exp(-0.3 * layer_idx)
    lp = lam_params.astype(jnp.float32)
    lam = jnp.exp(jnp.sum(lp[0] * lp[1])) - jnp.exp(jnp.sum(lp[2] * lp[3])) + lam_init
    oa = diff_attention(qa, ka, va, lam, A_QK_DIM ** -0.5)
    oa = rms_norm(oa, subln) * (1.0 - lam_init)
    qb = apply_rope(qb.reshape(b, s, B_HEADS, HEAD_DIM).transpose(0, 2, 1, 3), *rope128)
    kb = apply_rope(kb.reshape(b, s, B_HEADS, HEAD_DIM).transpose(0, 2, 1, 3), *rope128)
    vb = vb.reshape(b, s, B_HEADS, HEAD_DIM).transpose(0, 2, 1, 3)
    ob = dilated_window_attention(qb, kb, vb)
    o = jnp.concatenate([oa, ob], axis=1).transpose(0, 2, 1, 3).reshape(b, s, AB_OUT)
    return o @ w_out


def cd_mixer(h, w_in, q_norm, w_uq, kv_norm, w_ukv, w_out, rope64, rope128):
    b, s, _ = h.shape
    c_q, c_kv, k_rope, qd, kd, vd = split_cols(h @ w_in, CD_SPLITS)
    qc = (rms_norm(c_q, q_norm) @ w_uq).reshape(b, s, C_HEADS, C_NOPE + C_ROPE).transpose(0, 2, 1, 3)
    kv = (rms_norm(c_kv, kv_norm) @ w_ukv).reshape(b, s, C_HEADS, C_NOPE + C_V).transpose(0, 2, 1, 3)
    k_rope = apply_rope(k_rope, *rope64)[:, None]
    qc = jnp.concatenate([qc[..., :C_NOPE], apply_rope(qc[..., C_NOPE:], *rope64)], axis=-1)
    kc = jnp.concatenate([kv[..., :C_NOPE], jnp.broadcast_to(k_rope, (b, C_HEADS, s, C_ROPE))], axis=-1)
    oc = causal_attention(qc, kc, kv[..., C_NOPE:], (C_NOPE + C_ROPE) ** -0.5)
    qd = apply_rope(qd.reshape(b, s, D_HEADS, HEAD_DIM).transpose(0, 2, 1, 3), *rope128)
    kd = apply_rope(kd.reshape(b, s, D_HEADS, HEAD_DIM).transpose(0, 2, 1, 3), *rope128)
    vd = vd.reshape(b, s, D_HEADS, HEAD_DIM).transpose(0, 2, 1, 3)
    od = moba_attention(qd, kd, vd)
    o = jnp.concatenate([oc, od], axis=1).transpose(0, 2, 1, 3).reshape(b, s, CD_OUT)
    return o @ w_out


def setup_inputs(seed: int = 0) -> dict:
    key = jax.random.key(seed)
    ks = jax.random.split(key, 20)
    n_even = (DEPTH + 1) // 2
    n_odd = DEPTH // 2
    f32 = jnp.float32

    def dense(k, shape):
        return jax.random.normal(k, shape, f32) * (shape[-2] ** -0.5)

    def gain(k, shape):
        return 1.0 + 0.02 * jax.random.normal(k, shape, f32)

    return {
        'x': jax.random.normal(ks[0], (BATCH, SEQ, D_MODEL), f32),
        'p': jax.random.normal(ks[1], (DEPTH, BATCH, SEQ, PLE_DIM), f32),
        'ffn_norm': gain(ks[2], (DEPTH, 2, D_MODEL)),
        'ffn_w_gu': dense(ks[3], (DEPTH, 2, D_MODEL, 2 * D_FF)),
        'ffn_w_down': dense(ks[4], (DEPTH, 2, D_FF, D_MODEL)),
        'mix_norm': gain(ks[5], (DEPTH, D_MODEL)),
        'ab_w_in': dense(ks[6], (n_even, D_MODEL, AB_IN)),
        'ab_lambda': 0.1 * jax.random.normal(ks[7], (n_even, 4, A_QK_DIM), f32),
        'ab_subln': gain(ks[8], (n_even, A_V_DIM)),
        'ab_w_out': dense(ks[9], (n_even, AB_OUT, D_MODEL)),
        'cd_w_in': dense(ks[10], (n_odd, D_MODEL, CD_IN)),
        'cd_q_norm': gain(ks[11], (n_odd, C_Q_LORA)),
        'cd_w_uq': dense(ks[12], (n_odd, C_Q_LORA, C_HEADS * (C_NOPE + C_ROPE))),
        'cd_kv_norm': gain(ks[13], (n_odd, C_KV_LORA)),
        'cd_w_ukv': dense(ks[14], (n_odd, C_KV_LORA, C_HEADS * (C_NOPE + C_V))),
        'cd_w_out': dense(ks[15], (n_odd, CD_OUT, D_MODEL)),
        'ple_norm': gain(ks[16], (DEPTH, D_MODEL)),
        'ple_w_gate': dense(ks[17], (DEPTH, D_MODEL, D_MODEL)),
        'ple_w_proj': dense(ks[18], (DEPTH, PLE_DIM, D_MODEL)),
        'final_norm': gain(ks[19], (D_MODEL,)),
    }


def reference(x, p, ffn_norm, ffn_w_gu, ffn_w_down, mix_norm, ab_w_in, ab_lambda, ab_subln, ab_w_out,
              cd_w_in, cd_q_norm, cd_w_uq, cd_kv_norm, cd_w_ukv, cd_w_out, ple_norm, ple_w_gate,
              ple_w_proj, final_norm):
    s = x.shape[1]
    rope64 = rope_tables(s, 64)
    rope128 = rope_tables(s, HEAD_DIM)
    for i in range(DEPTH):
        j = i // 2
        x = x + 0.5 * swiglu(rms_norm(x, ffn_norm[i, 0]), ffn_w_gu[i, 0], ffn_w_down[i, 0])
        h = rms_norm(x, mix_norm[i])
        if i % 2 == 0:
            x = x + ab_mixer(h, ab_w_in[j], ab_lambda[j], ab_subln[j], ab_w_out[j], i, rope64, rope128)
        else:
            x = x + cd_mixer(h, cd_w_in[j], cd_q_norm[j], cd_w_uq[j], cd_kv_norm[j], cd_w_ukv[j],
                             cd_w_out[j], rope64, rope128)
        x = x + 0.5 * swiglu(rms_norm(x, ffn_norm[i, 1]), ffn_w_gu[i, 1], ffn_w_down[i, 1])
        gate = jax.nn.sigmoid(rms_norm(x, ple_norm[i]) @ ple_w_gate[i])
        x = x + gate * (p[i] @ ple_w_proj[i])
    return rms_norm(x, final_norm)
```

```python
import math
from contextlib import ExitStack

import numpy as np
import ml_dtypes

import concourse.bass as bass
import concourse.mybir as mybir
from concourse.bass_utils import run_bass_kernel_spmd

F32 = mybir.dt.float32
BF16 = mybir.dt.bfloat16
AF = mybir.ActivationFunctionType
ALU = mybir.AluOpType
NPBF = ml_dtypes.bfloat16

D = 2048
DFF = 5632
NCH = 16
NHC = 44
SEQ = 4096
EPS = 1e-6
NEG = -30000.0


class Tl:
    __slots__ = ("w", "r", "dsem", "dcnt", "name", "sw")

    def __init__(self, name=""):
        self.w = None
        self.r = {}
        self.dsem = None
        self.dcnt = 0
        self.name = name
        self.sw = False


class Eng:
    def __init__(self, name, h, sem):
        self.name = name
        self.h = h
        self.sem = sem
        self.cnt = 0
        self.seen = {}


class K:
    def __init__(self, nc, es):
        self.nc = nc
        self.es = es
        self.sem_es = es
        self.nsem = 0
        mk = lambda n, h: Eng(n, h, es.enter_context(nc.semaphore("s_" + n)))
        self.pe = mk("pe", nc.tensor)
        self.act = mk("act", nc.scalar)
        self.dve = mk("dve", nc.vector)
        self.pool = mk("pool", nc.gpsimd)
        self.sp = mk("sp", nc.sync)
        self.same_engine_sync = True
        self.prefix = ""
        self.dma_tiles = []
        self.free_sems = []
        self.free_sw = []
        self.ncoll = 0

    def barrier(self):
        engs = [self.pe, self.act, self.dve, self.pool, self.sp]
        for e in engs:
            for e2 in engs:
                if e2 is not e and e2.cnt > 0 and e.seen.get(id(e2.sem), 0) < e2.cnt:
                    e.h.wait_ge(e2.sem, e2.cnt)
                    e.seen[id(e2.sem)] = e2.cnt
            for t in self.dma_tiles:
                if e.seen.get(id(t.dsem), 0) < t.dcnt:
                    e.h.wait_ge(t.dsem, t.dcnt)
                    e.seen[id(t.dsem)] = t.dcnt
        for t in self.dma_tiles:
            (self.free_sw if t.sw else self.free_sems).append((t.dsem, t.dcnt))
            t.dsem = None
        self.dma_tiles = []

    def pair_gather(self, srcs, dsts, tiles=None):
        sems = []
        for src, dst in zip(srcs, dsts):
            csem = self.sem_es.enter_context(self.nc.semaphore("c%d" % self.ncoll))
            self.ncoll += 1
            self.nc.gpsimd.collective_compute("AllGather", ALU.bypass, replica_groups=[[0, 1], [2, 3], [4, 5], [6, 7]],
                                              ins=[src], outs=[dst]).then_inc(csem, 1)
            sems.append(csem)
        if tiles is not None:
            for t, csem in zip(tiles, sems):
                t.w = (csem, 1)
                t.r = {}
            return
        for e in [self.pe, self.act, self.dve, self.pool, self.sp]:
            for csem in sems:
                e.h.wait_ge(csem, 1)

    def sb(self, name, shape, dt):
        return self.es.enter_context(self.nc.sbuf_tensor("sb_" + self.prefix + name, shape, dt))

    def ps(self, name, shape, dt=F32):
        return self.es.enter_context(self.nc.psum_tensor("ps_" + self.prefix + name, shape, dt))

    def _waits(self, e, reads, writes, attach=False):
        deps = {}

        def need(tok):
            if tok is None:
                return
            s, c = tok
            if deps.get(id(s), (None, 0))[1] < c:
                deps[id(s)] = (s, c)

        for b in reads:
            need(b.w)
        for b in writes:
            need(b.w)
            for s, c in b.r.items():
                need((s, c))
        todo = []
        for s, c in deps.values():
            if s is e.sem and (e is self.pe or not self.same_engine_sync):
                continue
            if e.seen.get(id(s), 0) < c:
                todo.append((s, c))
                e.seen[id(s)] = c
        last = todo.pop() if (attach and todo) else None
        for s, c in todo:
            e.h.wait_ge(s, c)
        return last

    def op(self, e, fn, reads=(), writes=()):
        last = self._waits(e, reads, writes, attach=True)
        inst = fn(e.h)
        if last is not None:
            inst._wait_ge(last[0], last[1])
        e.cnt += 1
        inst.then_inc(e.sem, 1)
        for b in reads:
            b.r[e.sem] = e.cnt
        for b in writes:
            b.w = (e.sem, e.cnt)
            b.r = {}
        return inst

    def dma(self, q, out, in_, reads=(), writes=(), semt=None):
        if semt is None:
            semt = writes[0] if writes else reads[0]
        sw = q is self.pool
        if semt.dsem is None:
            pool_ = self.free_sw if sw else self.free_sems
            semt.sw = sw
            if pool_:
                semt.dsem, semt.dcnt = pool_.pop()
            else:
                semt.dsem = self.sem_es.enter_context(self.nc.semaphore("d%d" % self.nsem))
                self.nsem += 1
            self.dma_tiles.append(semt)
        self._waits(q, reads, writes)
        inst = q.h.dma_start(out=out, in_=in_)
        semt.dcnt += 16
        inst.then_inc(semt.dsem, 16)
        for b in reads:
            b.r[semt.dsem] = semt.dcnt
        for b in writes:
            b.w = (semt.dsem, semt.dcnt)
            b.r = {}
        return inst

    def wait_all(self, e, tiles):
        self._waits(e, tiles, tiles)


TT = 1024
NTH = TT // 512


class TState:
    def __init__(self, k):
        self.k = k
        self.xs = k.sb("xs", [128, NCH, TT], F32)
        self.hT = k.sb("hT", [128, NCH, TT], BF16)
        self.aT = k.sb("aT", [128, 22, TT], BF16)
        self.sq = k.sb("sq", [128, NCH, 512], BF16)
        self.rstd = k.sb("rstd", [128, 512], F32)
        self.sg = k.sb("sg", [128, 2, 512], F32)
        self.wgu = [k.sb("wgu%d" % i, [128, NCH, 256], BF16) for i in range(2)]
        self.wd = [k.sb("wd%d" % i, [128, 22, 128], BF16) for i in range(3)]
        self.pTs = k.sb("pTs", [128, 2, TT], BF16)
        self.wpj = k.sb("wpj", [128, 2, D], BF16)
        self.ones = k.sb("ones", [128, 128], BF16)
        self.gains = k.sb("gains", [128, 9 * NCH], F32)
        self.P = [k.ps("P%d" % i, [128, 512]) for i in range(8)]
        T = Tl
        self.t_xs = [[T("xs%d_%d" % (i, fc)) for fc in range(NCH)] for i in range(NTH)]
        self.t_hT = [T("hT%d" % i) for i in range(NTH)]
        self.t_aT = [T("aT%d" % i) for i in range(NTH)]
        self.t_sq = T("sq")
        self.t_sqf = [T("sq%d" % fc) for fc in range(NCH)]
        self.t_rstd = T("rstd")
        self.t_sg = [T("sg0"), T("sg1")]
        self.t_wgu = [T("wgu0"), T("wgu1")]
        self.t_wd = [T("wd0"), T("wd1"), T("wd2")]
        self.t_pT = T("pT")
        self.t_wpj = T("wpj")
        self.t_ones = T("ones")
        self.t_gains = T("gains")
        self.t_P = [T("P%d" % i) for i in range(8)]
        self.iwgu = 0
        self.iwd = 0
        self.igu = 0
        self.idn = 0
        k.op(k.pool, lambda h: h.memset(self.ones[:], 1.0), writes=[self.t_ones])


def t_norm(S, gcol, out=None, t_out=None):
    k = S.k
    for th in range(NTH):
        cs = slice(th * 512, (th + 1) * 512)
        for fc in range(NCH):
            k.op(k.act, lambda h, fc=fc: h.activation(out=S.sq[:, fc, :], in_=S.xs[:, fc, cs], func=AF.Square),
                 reads=[S.t_xs[th][fc]], writes=[S.t_sqf[fc]])
        pss, tpss = S.P[6 + th % 2], S.t_P[6 + th % 2]
        for fc in range(NCH):
            k.op(k.pe, lambda h, fc=fc: h.matmul(pss[:], S.ones[:], S.sq[:, fc, :], start=(fc == 0), stop=(fc == NCH - 1)),
                 reads=[S.t_sqf[fc], S.t_ones], writes=[tpss])
        k.op(k.act, lambda h: h.activation(out=S.rstd[:], in_=pss[:], func=AF.Sqrt, scale=1.0 / D, bias=EPS),
             reads=[tpss], writes=[S.t_rstd])
        k.op(k.dve, lambda h: h.reciprocal(out=S.rstd[:], in_=S.rstd[:]), reads=[S.t_rstd], writes=[S.t_rstd])
        for fc in range(NCH):
            if out is None:
                dst, tdst = S.hT[:, fc, cs], S.t_hT[th]
            else:
                dst, tdst = out[:, fc, cs], t_out[th][fc]
            k.op(k.dve, lambda h, fc=fc, dst=dst: h.scalar_tensor_tensor(
                out=dst, in0=S.xs[:, fc, cs], scalar=S.gains[:, gcol + fc:gcol + fc + 1], in1=S.rstd[:],
                op0=ALU.mult, op1=ALU.mult), reads=[S.t_xs[th][fc], S.t_rstd, S.t_gains], writes=[tdst])


def t_ffn(S, wgu_d, wd_d, gcol):
    k = S.k
    t_norm(S, gcol)
    for half in range(2):
        def load_gu(c):
            i = S.iwgu % 2
            S.iwgu += 1
            k.dma(k.pool, S.wgu[i][:], wgu_d[half * 22 + c], writes=[S.t_wgu[i]])
            return i
        slot = load_gu(0)
        for c in range(22):
            nslot = load_gu(c + 1) if c + 1 < 22 else None
            w, tw = S.wgu[slot], S.t_wgu[slot]
            for th in range(NTH):
                cs = slice(th * 512, (th + 1) * 512)
                u = S.igu % 2
                S.igu += 1
                pg, pu, tpg, tpu = S.P[2 * u], S.P[2 * u + 1], S.t_P[2 * u], S.t_P[2 * u + 1]
                for kc in range(NCH):
                    k.op(k.pe, lambda h, kc=kc: h.matmul(pg[:], w[:, kc, 0:128], S.hT[:, kc, cs], start=(kc == 0), stop=(kc == NCH - 1)),
                         reads=[tw, S.t_hT[th]], writes=[tpg])
                for kc in range(NCH):
                    k.op(k.pe, lambda h, kc=kc: h.matmul(pu[:], w[:, kc, 128:256], S.hT[:, kc, cs], start=(kc == 0), stop=(kc == NCH - 1)),
                         reads=[tw, S.t_hT[th]], writes=[tpu])
                sg, tsg = S.sg[:, u, :], S.t_sg[u]
                k.op(k.act, lambda h: h.activation(out=sg, in_=pg[:], func=AF.Silu), reads=[tpg], writes=[tsg])
                k.op(k.dve, lambda h, c=c: h.tensor_tensor(out=S.aT[:, c, cs], in0=sg, in1=pu[:], op=ALU.mult),
                     reads=[tsg, tpu], writes=[S.t_aT[th]])
            slot = nslot
        def load_d(oc):
            i = S.iwd % 3
            S.iwd += 1
            k.dma(k.pool, S.wd[i][:], wd_d[half, oc], writes=[S.t_wd[i]])
            return i
        slots = [load_d(0), load_d(1)]
        for oc in range(NCH):
            if oc + 2 < NCH:
                slots.append(load_d(oc + 2))
            w, tw = S.wd[slots[oc]], S.t_wd[slots[oc]]
            for th in range(NTH):
                cs = slice(th * 512, (th + 1) * 512)
                u = S.idn % 2
                S.idn += 1
                pd, tpd = S.P[4 + u], S.t_P[4 + u]
                for kc in range(22):
                    k.op(k.pe, lambda h, kc=kc: h.matmul(pd[:], w[:, kc, :], S.aT[:, kc, cs], start=(kc == 0), stop=(kc == 21)),
                         reads=[tw, S.t_aT[th]], writes=[tpd])
                k.op(k.dve, lambda h, oc=oc: h.scalar_tensor_tensor(
                    out=S.xs[:, oc, cs], in0=pd[:], scalar=0.5, in1=S.xs[:, oc, cs], op0=ALU.mult, op1=ALU.add),
                    reads=[tpd, S.t_xs[th][oc]], writes=[S.t_xs[th][oc]])


def t_proj_add(S, w_d, nkc, src, t_src, mode, gate_w_d=None, pj=None):
    k = S.k

    def load(oc):
        i = S.iwd % 3
        S.iwd += 1
        k.dma(k.pool, S.wd[i][:, 0:nkc, :], w_d[oc], writes=[S.t_wd[i]])
        return i
    slots = [load(0), load(1)]
    for oc in range(NCH):
        if oc + 2 < NCH:
            slots.append(load(oc + 2))
        w, tw = S.wd[slots[oc]], S.t_wd[slots[oc]]
        for th in range(NTH):
            cs = slice(th * 512, (th + 1) * 512)
            u = S.idn % 2
            S.idn += 1
            pd, tpd = S.P[4 + u], S.t_P[4 + u]
            for kc in range(nkc):
                k.op(k.pe, lambda h, kc=kc: h.matmul(pd[:], w[:, kc, :], src[:, kc, cs], start=(kc == 0), stop=(kc == nkc - 1)),
                     reads=[tw, t_src[th]], writes=[tpd])
            if mode == "add":
                k.op(k.dve, lambda h, oc=oc: h.tensor_tensor(out=S.xs[:, oc, cs], in0=pd[:], in1=S.xs[:, oc, cs], op=ALU.add),
                     reads=[tpd, S.t_xs[th][oc]], writes=[S.t_xs[th][oc]])
            else:
                pq, tpq = S.P[6 + u], S.t_P[6 + u]
                for kc in range(2):
                    k.op(k.pe, lambda h, kc=kc, oc=oc: h.matmul(pq[:], S.wpj[:, kc, oc * 128:(oc + 1) * 128], S.pTs[:, kc, cs],
                                                                start=(kc == 0), stop=(kc == 1)),
                         reads=[S.t_wpj, S.t_pT], writes=[tpq])
                sg, tsg = S.sg[:, u, :], S.t_sg[u]
                k.op(k.act, lambda h: h.activation(out=sg, in_=pd[:], func=AF.Sigmoid), reads=[tpd], writes=[tsg])
                k.op(k.dve, lambda h: h.tensor_tensor(out=sg, in0=sg, in1=pq[:], op=ALU.mult), reads=[tsg, tpq], writes=[tsg])
                k.op(k.dve, lambda h, oc=oc: h.tensor_tensor(out=S.xs[:, oc, cs], in0=sg, in1=S.xs[:, oc, cs], op=ALU.add),
                     reads=[tsg, S.t_xs[th][oc]], writes=[S.t_xs[th][oc]])


def t_phase(k, S, ntok, dr, steps, pre=None):
    t_x = Tl("xT_d")
    t_h = Tl("hT_d")
    t_out = Tl("out_d")
    k.dma(k.sp, S.gains[:], dr["gains"], writes=[S.t_gains])
    for tt in range(ntok // TT):
        ts = slice(tt * TT, (tt + 1) * TT)
        for th in range(NTH):
            k.dma(k.sp, S.xs[:, :, th * 512:(th + 1) * 512],
                  dr["xT"].rearrange("(c p) t -> p c t", p=128)[:, :, tt * TT + th * 512: tt * TT + (th + 1) * 512],
                  writes=S.t_xs[th])
        if tt == 0 and pre is not None:
            pre()
        for st in steps:
            kind = st[0]
            if kind == "ffn":
                t_ffn(S, dr[st[1]], dr[st[2]], st[3])
            elif kind == "oproj":
                nkc, gpfx, rows_half = st[2], st[3], st[4]
                sel, t_sel = dr["_sel"]
                nh = rows_half // 128
                na = 4
                nb = nh - na
                for th in range(NTH):
                    cs = slice(th * 512, (th + 1) * 512)
                    for kc in range(nkc):
                        if kc < na:
                            rr, lc = 0, kc
                        elif kc < 2 * na:
                            rr, lc = 1, kc - na
                        elif kc < 2 * na + nb:
                            rr, lc = 0, na + kc - 2 * na
                        else:
                            rr, lc = 1, na + kc - 2 * na - nb
                        og = dr[gpfx + "%d" % (lc // 2)]
                        r0 = rr * 256 + (lc % 2) * 128
                        c0 = tt * TT + th * 512
                        k.dma(k.sp, S.aT[:, kc, cs], og[r0:r0 + 128, c0:c0 + 512], writes=[S.t_aT[th]])
                        k.dma(k.sp, S.sq[:, kc, :], og[r0:r0 + 128, NTOK + c0:NTOK + c0 + 512], writes=[S.t_sqf[kc]])
                    k.op(k.dve, lambda h: h.tensor_scalar(out=S.hT[:, 0:nkc, cs], in0=S.aT[:, 0:nkc, cs], scalar1=sel[:, 0:1], scalar2=None, op0=ALU.mult),
                         reads=[S.t_aT[th], t_sel], writes=[S.t_hT[th]])
                    k.op(k.dve, lambda h: h.scalar_tensor_tensor(out=S.hT[:, 0:nkc, cs], in0=S.sq[:, 0:nkc, :], scalar=sel[:, 1:2], in1=S.hT[:, 0:nkc, cs],
                                                                 op0=ALU.mult, op1=ALU.add),
                         reads=S.t_sqf[0:nkc] + [S.t_hT[th], t_sel], writes=[S.t_hT[th]])
                t_proj_add(S, dr[st[1]], nkc, S.hT, S.t_hT, "add")
            elif kind == "ple":
                k.dma(k.pool, S.wpj[:], dr[st[2]], writes=[S.t_wpj])
                k.dma(k.pool, S.pTs[:], dr[st[3]].rearrange("(c p) t -> p c t", p=128)[:, :, ts], writes=[S.t_pT])
                t_norm(S, st[4])
                t_proj_add(S, dr[st[1]], NCH, S.hT, S.t_hT, "ple")
            elif kind == "hout":
                t_norm(S, st[1])
                for th in range(NTH):
                    k.dma(k.sp, dr["hm%d" % (tt * NTH + th)].rearrange("(c p) t -> p c t", p=128),
                          S.hT[:, :, th * 512:(th + 1) * 512], reads=[S.t_hT[th]], writes=[t_h], semt=t_h)
            elif kind == "final":
                t_norm(S, st[1], out=S.xs, t_out=S.t_xs)
        for th in range(NTH):
            k.dma(k.sp, dr["xT_out"].rearrange("(c p) t -> p c t", p=128)[:, :, tt * TT + th * 512: tt * TT + (th + 1) * 512],
                  S.xs[:, :, th * 512:(th + 1) * 512], reads=S.t_xs[th], writes=[t_out], semt=t_out)
    k.wait_all(k.sp, [t_out, t_h])


def lay_wgu(w):
    g = w[:, :DFF].reshape(NCH, 128, NHC, 128)
    u = w[:, DFF:].reshape(NCH, 128, NHC, 128)
    out = np.empty((NHC, 128, NCH, 256), np.float32)
    out[:, :, :, :128] = g.transpose(2, 1, 0, 3)
    out[:, :, :, 128:] = u.transpose(2, 1, 0, 3)
    return out


def lay_wd(w):
    a = w.reshape(2, 22, 128, NCH, 128)
    return np.ascontiguousarray(a.transpose(0, 3, 2, 1, 4))


def lay_wsq(w):
    nkc = w.shape[0] // 128
    a = w.reshape(nkc, 128, NCH, 128)
    return np.ascontiguousarray(a.transpose(2, 1, 0, 3))


def lay_rows(w):
    nkc = w.shape[0] // 128
    return np.ascontiguousarray(w.reshape(nkc, 128, w.shape[1]).transpose(1, 0, 2))


def lay_gains(vs):
    return np.ascontiguousarray(np.concatenate([v.reshape(NCH, 128).T for v in vs], axis=1)).astype(np.float32)


class HConst:
    def __init__(self, k, consts_d):
        self.k = k
        self.c = k.sb("hconst", [128, 1792 + 2048], BF16)
        self.t = Tl("hconst")
        k.dma(k.pool, self.c[:, 0:1792], consts_d, writes=[self.t])
        self.ident = self.c[:, 0:128]
        self.permA = self.c[:, 128:256]
        self.permB = self.c[:, 256:384]
        self.tri = self.c[:, 384:512]
        self.mask4 = self.c[:, 512:1024]
        self.mask4f = self.c[:, 1024:1536]
        self.ones = self.c[:, 1792:1920]
        k.op(k.pool, lambda h: h.memset(self.c[:, 1792:1920], 1.0), writes=[self.t])
        self.ones32 = k.sb("ones32", [128, 128], F32)
        k.op(k.pool, lambda h: h.memset(self.ones32[:], 1.0), writes=[self.t])
        self.P = [k.ps("HP%d" % i, [128, 512]) for i in range(8)]
        self.t_P = [Tl("HP%d" % i) for i in range(8)]

    def esel(self, n):
        return self.c[0:16, 1536 + n * 16:1536 + n * 16 + 16]


def make_consts():
    c = np.zeros((128, 1792), np.float32)
    idx = np.arange(128)
    c[idx, idx] = 1.0
    c[idx ^ 32, 128 + idx] = 1.0
    c[idx ^ 64, 256 + idx] = 1.0
    kk, qq = np.meshgrid(idx, idx, indexing="ij")
    le = (kk <= qq).astype(np.float32)
    ge = (kk >= qq).astype(np.float32)
    c[:, 384:512] = le
    c[:, 512:1024] = np.concatenate([ge, le, ge, le], axis=1)
    c[:, 1024:1536] = np.concatenate([np.zeros_like(ge), le, ge, le], axis=1)
    return c


def make_rope(dim, layout_rows, half):
    inv = (1.0 / (10000.0 ** (np.arange(0, dim, 2, dtype=np.float32) / np.float32(dim)))).astype(np.float32)
    ang = (np.arange(SEQ, dtype=np.float32)[:, None] * inv[None, :]).astype(np.float32)
    cos = np.cos(ang).astype(np.float32).T
    sin = np.sin(ang).astype(np.float32).T
    m = np.arange(layout_rows)
    j = m % half
    sign = np.where((m % (2 * half)) < half, -1.0, 1.0).astype(np.float32)
    return np.ascontiguousarray(np.stack([cos[j], sin[j] * sign[:, None]]))


class RopeUnit:
    def __init__(self, k, H):
        self.k, self.H = k, H
        self.xb = [k.sb("r_xb%d" % i, [128, 512], BF16) for i in range(2)]
        self.t1 = [k.sb("r_t1%d" % i, [128, 512], F32) for i in range(2)]
        self.t2 = [k.sb("r_t2%d" % i, [128, 512], F32) for i in range(2)]
        self.ost = [k.sb("r_os%d" % i, [128, 512], BF16) for i in range(3)]
        self.t_xb = [Tl() for _ in range(2)]
        self.t_t1 = [Tl() for _ in range(2)]
        self.t_t2 = [Tl() for _ in range(2)]
        self.t_ost = [Tl() for _ in range(3)]
        self.i = 0
        self.io = 0

    def store(self, ps, tps, rows, dst, t_dst, perm=None, C=None, S=None, t_tab=None):
        k, H = self.k, self.H
        o = self.io % 3
        self.io += 1
        ost, tost = self.ost[o], self.t_ost[o]
        if perm is None:
            k.op(k.act, lambda h: h.activation(out=ost[0:rows, :], in_=ps[0:rows, :], func=AF.Copy), reads=[tps], writes=[tost])

            def stage_b():
                k.dma(k.sp, dst, ost[0:rows, :], reads=[tost], writes=[t_dst], semt=t_dst)
        else:
            i = self.i % 2
            self.i += 1
            xb, t1, t2 = self.xb[i], self.t1[i], self.t2[i]
            k.op(k.act, lambda h: h.activation(out=xb[0:rows, :], in_=ps[0:rows, :], func=AF.Copy), reads=[tps], writes=[self.t_xb[i]])

            def stage_b():
                pw, tpw = H.P[6 + i], H.t_P[6 + i]
                k.op(k.pe, lambda h: h.matmul(pw[0:rows, :], perm[0:rows, 0:rows], xb[0:rows, :], start=True, stop=True),
                     reads=[self.t_xb[i], H.t], writes=[tpw])
                k.op(k.dve, lambda h: h.tensor_tensor(out=t1[0:rows, :], in0=xb[0:rows, :], in1=C, op=ALU.mult),
                     reads=[self.t_xb[i], t_tab], writes=[self.t_t1[i]])
                k.op(k.dve, lambda h: h.tensor_tensor(out=t2[0:rows, :], in0=pw[0:rows, :], in1=S, op=ALU.mult),
                     reads=[tpw, t_tab], writes=[self.t_t2[i]])
                k.op(k.pool, lambda h: h.tensor_tensor(out=ost[0:rows, :], in0=t1[0:rows, :], in1=t2[0:rows, :], op=ALU.add),
                     reads=[self.t_t1[i], self.t_t2[i]], writes=[tost])
                k.dma(k.sp, dst, ost[0:rows, :], reads=[tost], writes=[t_dst], semt=t_dst)
        self.flush()
        self.pending = stage_b

    def flush(self):
        p = getattr(self, "pending", None)
        self.pending = None
        if p is not None:
            p()


class AttnBufs:
    def __init__(self, k, n=3):
        self.pT = [k.sb("a_pT%d" % i, [128, 512], BF16) for i in range(n)]
        self.t_pT = [Tl() for _ in range(n)]
        self.n = n
        self.i = 0
        self.cnt = 0
        self.acc = [[[k.sb("a_acc%d%d%d" % (s_, e_, p_), [128, 512], F32) for p_ in range(2)] for e_ in range(2)] for s_ in range(2)]
        self.t_acc = [[[Tl() for p_ in range(2)] for e_ in range(2)] for s_ in range(2)]


def run_causal_tile(k, H, A, j, streams, sidx):
    nkb = 4 * j + 4
    steps = [(kb, si) for kb in range(nkb) for si in range(len(streams))]
    two = len(streams) == 2
    for si in range(len(streams)):
        k.op(k.pool, lambda h, si=si: h.memset(A.acc[si][0][j % 2][:], 0.0), writes=[A.t_acc[si][0][j % 2]])
    G = 2
    groups = [steps[i:i + G] for i in range(0, len(steps), G)]
    pend = None
    for gi in range(len(groups) + 1):
        cur = None
        if gi < len(groups):
            cur = []
            plist = []
            for (kb, si) in groups[gi]:
                st = streams[si]
                r = kb - 4 * j
                q_lo = max(0, r) * 128
                sp_i = sidx[A.cnt % len(sidx)]
                A.cnt += 1
                sp, tsp = H.P[sp_i], H.t_P[sp_i]
                plist.append((st["parts"](kb, q_lo), sp, tsp, q_lo))
                cur.append((kb, si, sp, tsp, q_lo, r))
            for pi in range(max(len(p[0]) for p in plist)):
                for (parts, sp, tsp, q_lo) in plist:
                    if pi < len(parts):
                        lhsT, rhs, rd = parts[pi]
                        k.op(k.pe, lambda h: h.matmul(sp[:, q_lo:512], lhsT, rhs, start=(pi == 0), stop=(pi == len(parts) - 1)),
                             reads=rd, writes=[tsp])
        if pend is not None:
            done = []
            for (pkb, psi, psp, ptsp, pq_lo, pr) in pend:
                st = streams[psi]
                b = A.i % A.n
                A.i += 1
                pT, tpT = A.pT[b], A.t_pT[b]
                k.op(k.act, lambda h: h.activation(out=pT[:, pq_lo:512], in_=psp[:, pq_lo:512], func=AF.Exp, scale=st["scale"]),
                     reads=[ptsp], writes=[tpT])
                if pr >= 0:
                    k.op(k.pool, lambda h: h.tensor_tensor(out=pT[:, pq_lo:pq_lo + 128], in0=pT[:, pq_lo:pq_lo + 128], in1=H.tri, op=ALU.mult),
                         reads=[tpT, H.t], writes=[tpT])
                done.append((pkb, psi, pq_lo, pT, tpT))
            for (pkb, psi, pq_lo, pT, tpT) in done:
                st = streams[psi]
                vl, vrd = st["v"](pkb)
                last = (pkb == nkb - 1)
                k.op(k.pe, lambda h: h.matmul(H.P[st["O"]][:, pq_lo:512], vl, pT[:, pq_lo:512], start=(pkb == 0), stop=last),
                     reads=[tpT] + vrd, writes=[H.t_P[st["O"]]])
            for (pkb, psi, pq_lo, pT, tpT) in done:
                st = streams[psi]
                if two and pkb % 3 == 0:
                    k.op(k.pe, lambda h: h.matmul(H.P[st["D"]][:, pq_lo:512], H.ones, pT[:, pq_lo:512], start=(pkb == 0), stop=False),
                         reads=[tpT, H.t], writes=[H.t_P[st["D"]]])
                else:
                    ac, tac = A.acc[psi][0][j % 2], A.t_acc[psi][0][j % 2]
                    k.op(k.dve, lambda h: h.tensor_tensor(out=ac[:, pq_lo:512], in0=ac[:, pq_lo:512], in1=pT[:, pq_lo:512], op=ALU.add),
                         reads=[tpT, tac], writes=[tac])
        pend = cur
    for si, st in enumerate(streams):
        k.op(k.pe, lambda h: h.matmul(H.P[st["D"]][:], H.ones32[:], A.acc[si][0][j % 2][:], start=(not two), stop=True),
             reads=[A.t_acc[si][0][j % 2], H.t], writes=[H.t_P[st["D"]]])


class WTiles:
    def __init__(self, k, w_sb, w_d, ncols, step):
        self.step = step
        self.tl = []
        for c0 in range(0, ncols, step):
            c1 = min(ncols, c0 + step)
            t = Tl()
            k.dma(k.pool, w_sb[:, :, c0:c1], w_d[:, :, c0:c1], writes=[t])
            self.tl.append(t)

    def t(self, c0, n):
        return self.tl[c0 // self.step:(c0 + n - 1) // self.step + 1]


def h0_phase(k, dr, gather=None, opfx="o0m"):
    nc = k.nc
    t_qk = [Tl("qk%d" % i) for i in range(20)]
    t_va = Tl("va")
    t_vb = Tl("vb")
    t_oT = Tl("oT")
    def hload(dst, tdst, tt):
        rk, i4 = tt // 4, tt % 4
        src = dr["hg%d" % i4][rk * D:(rk + 1) * D, :].rearrange("(c p) t -> p c t", p=128)
        k.dma(k.sp, dst[:], src, reads=[dr["_thg"][i4]], writes=[tdst])

    def odst(lc, j):
        return dr[opfx + "%d" % (lc // 2)][(lc % 2) * 128:(lc % 2 + 1) * 128, j * 512:(j + 1) * 512]
    with ExitStack() as es:
        k.es = es
        H = HConst(k, dr["consts"])
        with ExitStack() as es2:
            k.es = es2
            W = k.sb("h0w", [128, NCH, 3840], BF16)
            WT = WTiles(k, W, dr["w_in"], 3840, 640)
            if gather is not None:
                gather()
            hs = [k.sb("h0h%d" % i, [128, NCH, 512], BF16) for i in range(2)]
            t_hs = [Tl(), Tl()]
            tab = [k.sb("h0tab%d" % i, [128, 4, 512], F32) for i in range(2)]
            t_tab = [Tl(), Tl()]
            vst = [k.sb("h0vst%d" % i, [128, 512], BF16) for i in range(2)]
            t_vst = [Tl(), Tl()]
            R = RopeUnit(k, H)
            ivs = 0
            for tt in range(8):
                i = tt % 2
                cs = slice(tt * 512, (tt + 1) * 512)
                hload(hs[i], t_hs[i], tt)
                k.dma(k.sp, tab[i][:, 0:2, :], dr["ropeA"].rearrange("a p t -> p a t")[:, :, cs], writes=[t_tab[i]])
                k.dma(k.sp, tab[i][:, 2:4, :], dr["ropeB"].rearrange("a p t -> p a t")[:, :, cs], writes=[t_tab[i]])
                for gi in range(20):
                    ps, tps = H.P[gi % 2], H.t_P[gi % 2]
                    for kc in range(NCH):
                        k.op(k.pe, lambda h, kc=kc: h.matmul(ps[:], W[:, kc, gi * 128:(gi + 1) * 128], hs[i][:, kc, :],
                                                             start=(kc == 0), stop=(kc == NCH - 1)), reads=WT.t(gi * 128, 128) + [t_hs[i]], writes=[tps])
                    if gi < 8:
                        R.store(ps, tps, 128, dr["qk"][gi][:, cs], t_qk[gi], H.permA, tab[i][:, 0, :], tab[i][:, 1, :], t_tab[i])
                    else:
                        R.store(ps, tps, 128, dr["qk"][gi][:, cs], t_qk[gi], H.permB, tab[i][:, 2, :], tab[i][:, 3, :], t_tab[i])
                for tb in range(4):
                    for (c0, ncol, dst, tdst) in ((2560, 512, dr["va"], t_va), (3072, 512, None, t_vb), (3584, 256, None, t_vb)):
                        ps, tps = H.P[2 + ivs % 2], H.t_P[2 + ivs % 2]
                        for kc in range(NCH):
                            k.op(k.pe, lambda h, kc=kc: h.matmul(ps[:, 0:ncol], hs[i][:, kc, tb * 128:(tb + 1) * 128], W[:, kc, c0:c0 + ncol],
                                                                 start=(kc == 0), stop=(kc == NCH - 1)), reads=WT.t(c0, ncol) + [t_hs[i]], writes=[tps])
                        v, tv = vst[ivs % 2], t_vst[ivs % 2]
                        ivs += 1
                        k.op(k.dve, lambda h: h.tensor_copy(out=v[:, 0:ncol], in_=ps[:, 0:ncol]), reads=[tps], writes=[tv])
                        rows = slice(tt * 512 + tb * 128, tt * 512 + (tb + 1) * 128)
                        if dst is not None:
                            k.dma(k.sp, dst[rows, :], v[:, 0:512], reads=[tv], writes=[tdst], semt=tdst)
                        elif ncol == 512:
                            k.dma(k.sp, dr["vb"][0][rows, :], v[:, 0:256], reads=[tv], writes=[tdst], semt=tdst)
                            k.dma(k.sp, dr["vb"][1][rows, :], v[:, 256:512], reads=[tv], writes=[tdst], semt=tdst)
                        else:
                            k.dma(k.sp, dr["vb"][2][rows, :], v[:, 0:256], reads=[tv], writes=[tdst], semt=tdst)
            R.flush()
            k.barrier()
        with ExitStack() as es2:
            k.es = es2
            A = AttnBufs(k, 4)
            qT = [k.sb("h0q%d" % i, [128, SEQ], BF16) for i in range(2)]
            kT = [k.sb("h0k%d" % i, [128, SEQ], BF16) for i in range(2)]
            vv = [k.sb("h0v%d" % i, [128, 32, 128], BF16) for i in range(2)]
            t_q = [Tl(), Tl()]
            t_k = [Tl(), Tl()]
            t_v = [Tl(), Tl()]
            lam = k.sb("h0lam", [128, 256], F32)
            lam2 = k.sb("h0lam2", [128, 8], F32)
            sub = k.sb("h0sub", [128, 1], F32)
            t_lam = Tl()
            k.dma(k.sp, lam[:], dr["lamb"], writes=[t_lam])
            k.dma(k.sp, sub[:], dr["subln"], writes=[t_lam])
            k.op(k.dve, lambda h: h.tensor_tensor(out=lam[:, 0:64], in0=lam[:, 0:64], in1=lam[:, 64:128], op=ALU.mult), reads=[t_lam], writes=[t_lam])
            k.op(k.dve, lambda h: h.tensor_tensor(out=lam[:, 128:192], in0=lam[:, 128:192], in1=lam[:, 192:256], op=ALU.mult), reads=[t_lam], writes=[t_lam])
            k.op(k.dve, lambda h: h.reduce_sum(out=lam2[:, 0:1], in_=lam[:, 0:64], axis=mybir.AxisListType.X), reads=[t_lam], writes=[t_lam])
            k.op(k.dve, lambda h: h.reduce_sum(out=lam2[:, 1:2], in_=lam[:, 128:192], axis=mybir.AxisListType.X), reads=[t_lam], writes=[t_lam])
            k.op(k.act, lambda h: h.activation(out=lam2[:, 2:4], in_=lam2[:, 0:2], func=AF.Exp), reads=[t_lam], writes=[t_lam])
            k.op(k.dve, lambda h: h.tensor_tensor(out=lam2[:, 4:5], in0=lam2[:, 3:4], in1=lam2[:, 2:3], op=ALU.subtract), reads=[t_lam], writes=[t_lam])
            k.op(k.dve, lambda h: h.tensor_scalar(out=lam2[:, 4:5], in0=lam2[:, 4:5], scalar1=-0.2, scalar2=1.0, op0=ALU.add, op1=ALU.mult), reads=[t_lam], writes=[t_lam])
            k.op(k.dve, lambda h: h.tensor_scalar(out=sub[:], in0=sub[:], scalar1=0.8, scalar2=0.0, op0=ALU.mult, op1=ALU.add), reads=[t_lam], writes=[t_lam])
            neglam = lam2[:, 4:5]
            rc = [k.sb("h0rc%d" % i, [128, 512], F32) for i in range(2)]
            oo = [k.sb("h0oo%d" % i, [128, 512], F32) for i in range(2)]
            sqb = k.sb("h0sq", [128, 512], BF16)
            ost = [k.sb("h0ost%d" % i, [128, 512], BF16) for i in range(2)]
            t_rc = [Tl(), Tl()]
            t_oo = [Tl(), Tl()]
            t_sq = Tl()
            t_ost = [Tl(), Tl()]
            io = 0

            def load_head(slot, gq, gk, vsrc):
                k.dma(k.sp, qT[slot][:], dr["qk"][gq], reads=[t_qk[gq]], writes=[t_q[slot]])
                k.dma(k.sp, kT[slot][:], dr["qk"][gk], reads=[t_qk[gk]], writes=[t_k[slot]])
                k.dma(k.sp, vv[slot][:], vsrc, reads=[t_va, t_vb], writes=[t_v[slot]])

            va_v = dr["va"].rearrange("(b p) c -> p b c", p=128)
            load_head(0, 0, 1, va_v[:, :, 0:128])
            for al in range(4):
                s = al % 2
                if al + 1 < 4:
                    load_head((al + 1) % 2, 2 * (al + 1), 2 * (al + 1) + 1, va_v[:, :, (al + 1) * 128:(al + 2) * 128])
                for j in range(8):
                    streams = []
                    for m in range(2):
                        rows = slice(m * 64, (m + 1) * 64)
                        streams.append(dict(
                            parts=lambda kb, q_lo, rows=rows: [(kT[s][rows, kb * 128:(kb + 1) * 128], qT[s][rows, j * 512 + q_lo:(j + 1) * 512], [t_k[s], t_q[s]])],
                            v=lambda kb: (vv[s][:, kb, :], [t_v[s]]), scale=0.125, O=3 + m, D=5 + m))
                    run_causal_tile(k, H, A, j, streams, [0, 1, 2, 7])
                    for m in range(2):
                        k.op(k.dve, lambda h, m=m: h.reciprocal(out=rc[m][:], in_=H.P[5 + m][:]), reads=[H.t_P[5 + m]], writes=[t_rc[m]])
                        k.op(k.dve, lambda h, m=m: h.tensor_tensor(out=oo[m][:], in0=H.P[3 + m][:], in1=rc[m][:], op=ALU.mult),
                             reads=[H.t_P[3 + m], t_rc[m]], writes=[t_oo[m]])
                    k.op(k.dve, lambda h: h.scalar_tensor_tensor(out=oo[0][:], in0=oo[1][:], scalar=neglam, in1=oo[0][:], op0=ALU.mult, op1=ALU.add),
                         reads=[t_oo[0], t_oo[1], t_lam], writes=[t_oo[0]])
                    k.op(k.act, lambda h: h.activation(out=sqb[:], in_=oo[0][:], func=AF.Square), reads=[t_oo[0]], writes=[t_sq])
                    k.op(k.pe, lambda h: h.matmul(H.P[5][:], H.ones, sqb[:], start=True, stop=True), reads=[t_sq, H.t], writes=[H.t_P[5]])
                    k.op(k.act, lambda h: h.activation(out=rc[0][:], in_=H.P[5][:], func=AF.Sqrt, scale=1.0 / 128, bias=EPS),
                         reads=[H.t_P[5]], writes=[t_rc[0]])
                    k.op(k.dve, lambda h: h.reciprocal(out=rc[0][:], in_=rc[0][:]), reads=[t_rc[0]], writes=[t_rc[0]])
                    o_ = ost[io % 2]
                    to_ = t_ost[io % 2]
                    io += 1
                    k.op(k.dve, lambda h: h.scalar_tensor_tensor(out=o_[:], in0=oo[0][:], scalar=sub[:, 0:1], in1=rc[0][:], op0=ALU.mult, op1=ALU.mult),
                         reads=[t_oo[0], t_rc[0], t_lam], writes=[to_])
                    k.dma(k.sp, odst(al, j), o_[:], reads=[to_], writes=[t_oT], semt=t_oT)
            num = k.sb("h0num", [128, SEQ], F32)
            den = k.sb("h0den", [128, SEQ], F32)
            t_num = Tl()
            t_den = Tl()
            DILS = (1, 4, 16)
            seqi = 0
            combos = [(ml, g) for ml in range(2) for g in range(3)]

            def load_dil(ci):
                ml, g = combos[ci]
                d = DILS[g]
                gh = 2 * g + ml
                slot = ci % 2
                k.dma(k.sp, qT[slot][:], dr["qk"][8 + 2 * gh], reads=[t_qk[8 + 2 * gh]], writes=[t_q[slot]])
                k.dma(k.sp, kT[slot][:], dr["qk"][8 + 2 * gh + 1], reads=[t_qk[8 + 2 * gh + 1]], writes=[t_k[slot]])
                nb = 32 // d
                for r in range(d):
                    src = dr["vb"][g][:, ml * 128:(ml + 1) * 128].rearrange("(kb i dd) c -> dd i kb c", dd=d, i=128)[r]
                    k.dma(k.sp, vv[slot][:, r * nb:(r + 1) * nb, :], src, reads=[t_vb], writes=[t_v[slot]])
            load_dil(0)
            for ci, (ml, g) in enumerate(combos):
                s = ci % 2
                if ci + 1 < len(combos):
                    load_dil(ci + 1)
                d = DILS[g]
                nb = 32 // d
                scale = 128 ** -0.5
                units = [(r, u) for r in range(d) for u in range(nb // 2)]
                pendu = None
                for ui in range(len(units) + 1):
                    curu = None
                    if ui < len(units):
                        r, u = units[ui]

                        def qsl(b, r=r):
                            st0 = r + d * 128 * b
                            return slice(st0, st0 + d * 127 + 1, d)
                        sp_i = seqi % 3
                        seqi += 1
                        sp, tsp = H.P[sp_i], H.t_P[sp_i]
                        kbs = [max(2 * u - 1, 0), 2 * u, 2 * u, 2 * u + 1]
                        qbs = [2 * u, 2 * u, 2 * u + 1, 2 * u + 1]
                        for qi in range(4):
                            k.op(k.pe, lambda h, qi=qi: h.matmul(sp[:, qi * 128:(qi + 1) * 128], kT[s][:, qsl(kbs[qi])], qT[s][:, qsl(qbs[qi])],
                                                                 start=True, stop=True), reads=[t_k[s], t_q[s]], writes=[tsp])
                        curu = (r, u, sp, tsp, kbs, seqi)
                    if pendu is not None:
                        r, u, sp, tsp, kbs, sq_ = pendu
                        b = A.i % A.n
                        A.i += 1
                        pT, tpT = A.pT[b], A.t_pT[b]
                        k.op(k.act, lambda h: h.activation(out=pT[:], in_=sp[:], func=AF.Exp, scale=scale), reads=[tsp], writes=[tpT])
                        mk = H.mask4f if u == 0 else H.mask4
                        k.op(k.pool, lambda h: h.tensor_tensor(out=pT[:], in0=pT[:], in1=mk, op=ALU.mult), reads=[tpT, H.t], writes=[tpT])
                        po, tpo = H.P[3 + (sq_ % 2)], H.t_P[3 + (sq_ % 2)]
                        pd, tpd = H.P[5 + (sq_ % 2)], H.t_P[5 + (sq_ % 2)]
                        for qi in range(4):
                            oc = slice((qi // 2) * 128, (qi // 2 + 1) * 128)
                            k.op(k.pe, lambda h, qi=qi, oc=oc: h.matmul(po[:, oc], vv[s][:, r * nb + kbs[qi], :], pT[:, qi * 128:(qi + 1) * 128],
                                                                        start=(qi % 2 == 0), stop=(qi % 2 == 1)), reads=[tpT, t_v[s]], writes=[tpo])
                        for qi in range(4):
                            oc = slice((qi // 2) * 128, (qi // 2 + 1) * 128)
                            k.op(k.pe, lambda h, qi=qi, oc=oc: h.matmul(pd[:, oc], H.ones, pT[:, qi * 128:(qi + 1) * 128],
                                                                        start=(qi % 2 == 0), stop=(qi % 2 == 1)), reads=[tpT, H.t], writes=[tpd])
                        st0 = r + d * 256 * u
                        qcols = slice(st0, st0 + d * 255 + 1, d)
                        if g == 0:
                            k.op(k.dve, lambda h: h.tensor_copy(out=num[:, qcols], in_=po[:, 0:256]), reads=[tpo], writes=[t_num])
                            k.op(k.dve, lambda h: h.tensor_copy(out=den[:, qcols], in_=pd[:, 0:256]), reads=[tpd], writes=[t_den])
                        else:
                            k.op(k.dve, lambda h: h.tensor_tensor(out=num[:, qcols], in0=po[:, 0:256], in1=num[:, qcols], op=ALU.add),
                                 reads=[tpo, t_num], writes=[t_num])
                            k.op(k.dve, lambda h: h.tensor_tensor(out=den[:, qcols], in0=pd[:, 0:256], in1=den[:, qcols], op=ALU.add),
                                 reads=[tpd, t_den], writes=[t_den])
                    pendu = curu
                if g == 2:
                    for jj in range(8):
                        cs = slice(jj * 512, (jj + 1) * 512)
                        k.op(k.dve, lambda h: h.reciprocal(out=rc[0][:], in_=den[:, cs]), reads=[t_den], writes=[t_rc[0]])
                        o_ = ost[io % 2]
                        to_ = t_ost[io % 2]
                        io += 1
                        k.op(k.dve, lambda h: h.tensor_tensor(out=o_[:], in0=num[:, cs], in1=rc[0][:], op=ALU.mult),
                             reads=[t_num, t_rc[0]], writes=[to_])
                        k.dma(k.sp, odst(4 + ml, jj), o_[:], reads=[to_], writes=[t_oT], semt=t_oT)
            k.wait_all(k.sp, [t_oT])
            k.barrier()
    k.es = None


def h0_cols(r):
    cols = []
    for al in range(4):
        a = 4 * r + al
        cols += list(range(128 * a, 128 * a + 128))
        cols += list(range(1024 + 128 * a, 1024 + 128 * a + 128))
    for g in range(3):
        for ml in range(2):
            hi = 4 * g + 2 * r + ml
            cols += list(range(3072 + 128 * hi, 3072 + 128 * hi + 128))
            cols += list(range(4608 + 128 * hi, 4608 + 128 * hi + 128))
    for al in range(4):
        a = 4 * r + al
        cols += list(range(2048 + 128 * a, 2048 + 128 * a + 128))
    for g in range(3):
        for ml in range(2):
            hi = 4 * g + 2 * r + ml
            cols += list(range(6144 + 128 * hi, 6144 + 128 * hi + 128))
    return np.array(cols)


def h1_phase(k, dr, gather=None, opfx="o1m"):
    t_qn = [Tl() for _ in range(4)]
    t_qr = [Tl() for _ in range(4)]
    t_kn = [Tl() for _ in range(4)]
    t_kr = Tl()
    t_vc = Tl()
    t_qkd = [Tl() for _ in range(8)]
    t_vd = Tl()
    t_oT = Tl("oT")
    def hload(dst, tdst, tt):
        rk, i4 = tt // 4, tt % 4
        src = dr["hg%d" % i4][rk * D:(rk + 1) * D, :].rearrange("(c p) t -> p c t", p=128)
        k.dma(k.sp, dst[:], src, reads=[dr["_thg"][i4]], writes=[tdst])

    def odst(lc, j):
        return dr[opfx + "%d" % (lc // 2)][(lc % 2) * 128:(lc % 2 + 1) * 128, j * 512:(j + 1) * 512]
    with ExitStack() as es:
        k.es = es
        H = HConst(k, dr["consts"])
        with ExitStack() as es2:
            k.es = es2
            W = k.sb("h1w", [128, NCH, 2368], BF16)
            Wq = k.sb("h1wq", [128, 4, 768], BF16)
            Wkv = k.sb("h1wkv", [128, 2, 1024], BF16)
            ng = k.sb("h1ng", [128, 6], F32)
            t_W = Tl("h1w")
            WT = WTiles(k, W, dr["w_in"], 2368, 592)
            k.dma(k.pool, Wq[:], dr["w_uq"], writes=[t_W])
            k.dma(k.pool, Wkv[:], dr["w_ukv"], writes=[t_W])
            k.dma(k.pool, ng[:], dr["ng"], writes=[t_W])
            if gather is not None:
                gather()
            hs = [k.sb("h1h%d" % i, [128, NCH, 512], BF16) for i in range(2)]
            t_hs = [Tl(), Tl()]
            tab = [k.sb("h1tab%d" % i, [128, 4, 512], F32) for i in range(2)]
            t_tab = [Tl(), Tl()]
            vst = [k.sb("h1vst%d" % i, [128, 512], BF16) for i in range(2)]
            t_vst = [Tl(), Tl()]
            cl = k.sb("h1cl", [128, 6, 512], F32)
            sq = k.sb("h1sq", [128, 6, 512], BF16)
            cn = k.sb("h1cn", [128, 6, 512], BF16)
            rs = k.sb("h1rs", [128, 2, 512], F32)
            t_cl, t_sq, t_cn, t_rs = Tl(), Tl(), Tl(), Tl()
            R = RopeUnit(k, H)
            ivs = 0
            ip = 0
            for tt in range(8):
                i = tt % 2
                cs = slice(tt * 512, (tt + 1) * 512)
                hload(hs[i], t_hs[i], tt)
                k.dma(k.sp, tab[i][:, 0:2, :], dr["ropeA"].rearrange("a p t -> p a t")[:, :, cs], writes=[t_tab[i]])
                k.dma(k.sp, tab[i][:, 2:4, :], dr["ropeB"].rearrange("a p t -> p a t")[:, :, cs], writes=[t_tab[i]])

                def proj(c0, M, lhs=None, rhs=None, nk=NCH, rd=None):
                    nonlocal ip
                    ps, tps = H.P[ip % 2], H.t_P[ip % 2]
                    ip += 1
                    for kc in range(nk):
                        if lhs is None:
                            l_, r_, rd_ = W[:, kc, c0:c0 + M], hs[i][:, kc, :], WT.t(c0, M) + [t_hs[i]]
                        else:
                            l_, r_, rd_ = lhs[:, kc, c0:c0 + M], rhs[:, kc, :], rd
                        k.op(k.pe, lambda h, l_=l_, r_=r_, kc=kc: h.matmul(ps[0:M, :], l_, r_, start=(kc == 0), stop=(kc == nk - 1)),
                             reads=rd_, writes=[tps])
                    return ps, tps
                for c in range(6):
                    ps, tps = proj(c * 128, 128)
                    k.op(k.act, lambda h: h.activation(out=cl[:, c, :], in_=ps[:], func=AF.Copy), reads=[tps], writes=[t_cl])
                    k.op(k.act, lambda h: h.activation(out=sq[:, c, :], in_=cl[:, c, :], func=AF.Square), reads=[t_cl], writes=[t_sq])
                for (c0, nc_, which) in ((0, 4, 0), (4, 2, 1)):
                    pss, tpss = H.P[4], H.t_P[4]
                    for c in range(nc_):
                        k.op(k.pe, lambda h, c=c: h.matmul(pss[:], H.ones, sq[:, c0 + c, :], start=(c == 0), stop=(c == nc_ - 1)),
                             reads=[t_sq, H.t], writes=[tpss])
                    k.op(k.act, lambda h: h.activation(out=rs[:, which, :], in_=pss[:], func=AF.Sqrt, scale=1.0 / (nc_ * 128), bias=EPS),
                         reads=[tpss], writes=[t_rs])
                    k.op(k.dve, lambda h: h.reciprocal(out=rs[:, which, :], in_=rs[:, which, :]), reads=[t_rs], writes=[t_rs])
                    for c in range(nc_):
                        k.op(k.dve, lambda h, c=c: h.scalar_tensor_tensor(out=cn[:, c0 + c, :], in0=cl[:, c0 + c, :], scalar=ng[:, c0 + c:c0 + c + 1],
                                                                       in1=rs[:, which, :], op0=ALU.mult, op1=ALU.mult),
                             reads=[t_cl, t_rs, t_W], writes=[t_cn])
                ps, tps = proj(768, 64)
                R.store(ps, tps, 64, dr["kr"][:, cs], t_kr, H.permA, tab[i][0:64, 0, :], tab[i][0:64, 1, :], t_tab[i])
                for hl in range(4):
                    ps, tps = proj(hl * 192, 128, Wq, cn[:, 0:4, :], 4, [t_W, t_cn])
                    R.store(ps, tps, 128, dr["qn"][hl][:, cs], t_qn[hl])
                    ps, tps = proj(hl * 192 + 128, 64, Wq, cn[:, 0:4, :], 4, [t_W, t_cn])
                    R.store(ps, tps, 64, dr["qr"][hl][:, cs], t_qr[hl], H.permA, tab[i][0:64, 0, :], tab[i][0:64, 1, :], t_tab[i])
                    ps, tps = proj(hl * 128, 128, Wkv, cn[:, 4:6, :], 2, [t_W, t_cn])
                    R.store(ps, tps, 128, dr["kn"][hl][:, cs], t_kn[hl])
                for gi in range(8):
                    ps, tps = proj(832 + gi * 128, 128)
                    R.store(ps, tps, 128, dr["qkd"][gi][:, cs], t_qkd[gi], H.permB, tab[i][:, 2, :], tab[i][:, 3, :], t_tab[i])
                for tb in range(4):
                    rows = slice(tt * 512 + tb * 128, tt * 512 + (tb + 1) * 128)
                    for which in range(2):
                        ps, tps = H.P[2 + ivs % 2], H.t_P[2 + ivs % 2]
                        if which == 0:
                            for kc in range(2):
                                k.op(k.pe, lambda h, kc=kc: h.matmul(ps[:], cn[:, 4 + kc, tb * 128:(tb + 1) * 128], Wkv[:, kc, 512:1024],
                                                                     start=(kc == 0), stop=(kc == 1)), reads=[t_W, t_cn], writes=[tps])
                        else:
                            for kc in range(NCH):
                                k.op(k.pe, lambda h, kc=kc: h.matmul(ps[:], hs[i][:, kc, tb * 128:(tb + 1) * 128], W[:, kc, 1856:2368],
                                                                     start=(kc == 0), stop=(kc == NCH - 1)), reads=WT.t(1856, 512) + [t_hs[i]], writes=[tps])
                        v, tv = vst[ivs % 2], t_vst[ivs % 2]
                        ivs += 1
                        k.op(k.dve, lambda h: h.tensor_copy(out=v[:], in_=ps[:]), reads=[tps], writes=[tv])
                        dst, tdst = (dr["vc"], t_vc) if which == 0 else (dr["vd"], t_vd)
                        k.dma(k.sp, dst[rows, :], v[:], reads=[tv], writes=[tdst], semt=tdst)
            R.flush()
            k.barrier()
        with ExitStack() as es2:
            k.es = es2
            A = AttnBufs(k, 4)
            qT = [k.sb("h1q%d" % i, [128, SEQ], BF16) for i in range(2)]
            kT = [k.sb("h1k%d" % i, [128, SEQ], BF16) for i in range(2)]
            qr = [k.sb("h1qr%d" % i, [128, SEQ], BF16) for i in range(2)]
            kr = k.sb("h1kr", [128, SEQ], BF16)
            vv = [k.sb("h1v%d" % i, [128, 32, 128], BF16) for i in range(2)]
            t_q, t_k, t_v, t_qrs = [Tl(), Tl()], [Tl(), Tl()], [Tl(), Tl()], [Tl(), Tl()]
            t_krs = Tl()
            rc = k.sb("h1rc", [128, 512], F32)
            t_rc = Tl()
            ost = [k.sb("h1ost%d" % i, [128, 512], BF16) for i in range(2)]
            t_ost = [Tl(), Tl()]
            io = 0
            k.dma(k.sp, kr[0:64, :], dr["kr"], reads=[t_kr], writes=[t_krs])
            k.dma(k.sp, kr[64:128, :], dr["kr"], reads=[t_kr], writes=[t_krs])
            vc_v = dr["vc"].rearrange("(b p) c -> p b c", p=128)
            vd_v = dr["vd"].rearrange("(b p) c -> p b c", p=128)

            def load_c(hl):
                s = hl % 2
                k.dma(k.sp, qT[s][:], dr["qn"][hl], reads=[t_qn[hl]], writes=[t_q[s]])
                k.dma(k.sp, qr[s][0:64, :], dr["qr"][hl], reads=[t_qr[hl]], writes=[t_qrs[s]])
                k.dma(k.sp, qr[s][64:128, :], dr["qr"][hl], reads=[t_qr[hl]], writes=[t_qrs[s]])
                k.dma(k.sp, kT[s][:], dr["kn"][hl], reads=[t_kn[hl]], writes=[t_k[s]])
                k.dma(k.sp, vv[s][:], vc_v[:, :, hl * 128:(hl + 1) * 128], reads=[t_vc], writes=[t_v[s]])

            def load_d(dl):
                s = dl % 2
                k.dma(k.sp, qT[s][:], dr["qkd"][2 * dl], reads=[t_qkd[2 * dl]], writes=[t_q[s]])
                k.dma(k.sp, kT[s][:], dr["qkd"][2 * dl + 1], reads=[t_qkd[2 * dl + 1]], writes=[t_k[s]])
                k.dma(k.sp, vv[s][:], vd_v[:, :, dl * 128:(dl + 1) * 128], reads=[t_vd], writes=[t_v[s]])

            def finish(j, row0):
                nonlocal io
                Oi, Di = 3 + 2 * (j % 2), 4 + 2 * (j % 2)
                k.op(k.dve, lambda h: h.reciprocal(out=rc[:], in_=H.P[Di][:]), reads=[H.t_P[Di]], writes=[t_rc])
                o_, to_ = ost[io % 2], t_ost[io % 2]
                io += 1
                k.op(k.dve, lambda h: h.tensor_tensor(out=o_[:], in0=H.P[Oi][:], in1=rc[:], op=ALU.mult), reads=[H.t_P[Oi], t_rc], writes=[to_])
                k.dma(k.sp, odst(row0 // 128, j), o_[:], reads=[to_], writes=[t_oT], semt=t_oT)
            load_c(0)
            for hl in range(4):
                s = hl % 2
                if hl + 1 < 4:
                    load_c(hl + 1)
                for j in range(8):
                    st = dict(parts=lambda kb, q_lo: [(kT[s][:, kb * 128:(kb + 1) * 128], qT[s][:, j * 512 + q_lo:(j + 1) * 512], [t_k[s], t_q[s]]),
                                                     (kr[64 * (kb % 2):64 * (kb % 2) + 64, kb * 128:(kb + 1) * 128],
                                                      qr[s][64 * (kb % 2):64 * (kb % 2) + 64, j * 512 + q_lo:(j + 1) * 512], [t_krs, t_qrs[s]])],
                              v=lambda kb: (vv[s][:, kb, :], [t_v[s]]), scale=192 ** -0.5, O=3 + 2 * (j % 2), D=4 + 2 * (j % 2))
                    run_causal_tile(k, H, A, j, [st], [0, 1, 2, 7])
                    finish(j, hl * 128)
            esel = k.sb("h1esel", [16, 2048], BF16)
            t_esel = Tl()
            k.dma(k.pool, esel[:], dr["esel"], writes=[t_esel])
            km32 = k.sb("h1km32", [128, 16], F32)
            km = k.sb("h1km", [128, 16], BF16)
            gm = k.sb("h1gm", [128, 32, 16], F32)
            mx = k.sb("h1mx", [128, 32, 8], F32)
            nm = k.sb("h1nm", [128, 32, 16], BF16)
            t_gmq = [Tl() for _ in range(32)]
            t_mxq = [Tl() for _ in range(32)]
            t_nmq = [Tl() for _ in range(32)]
            t_pgq = [Tl() for _ in range(32)]
            t_pnq = [Tl() for _ in range(8)]
            negT = k.sb("h1negT", [16, SEQ], BF16)
            t_km, t_gm, t_mx, t_nm, t_neg = Tl(), Tl(), Tl(), Tl(), Tl()
            load_d(0)
            for dl in range(4):
                s = dl % 2
                if dl + 1 < 4:
                    load_d(dl + 1)
                k.op(k.dve, lambda h: h.reduce_sum(out=km32[:], in_=kT[s][:].rearrange("p (n l) -> p n l", l=256), axis=mybir.AxisListType.X),
                     reads=[t_k[s]], writes=[t_km])
                k.op(k.dve, lambda h: h.tensor_scalar(out=km[:], in0=km32[:], scalar1=1.0 / 256, scalar2=None, op0=ALU.mult), reads=[t_km], writes=[t_km])
                k.op(k.pool, lambda h: h.memset(gm[:], -1e30), writes=t_gmq)
                k.op(k.pool, lambda h: h.memset(nm[:], 0.0), writes=t_nmq)
                k.op(k.pool, lambda h: h.memset(negT[:], 0.0), writes=[t_neg])
                pg, tpg = H.P[7], H.t_P[7]
                for qb in range(2, 32):
                    k.op(k.pe, lambda h: h.matmul(pg[:, qb * 16:(qb + 1) * 16], qT[s][:, qb * 128:(qb + 1) * 128], km[:], start=True, stop=True),
                         reads=[t_q[s], t_km], writes=[tpg])
                for qb in range(2, 32):
                    own = qb // 2
                    k.op(k.dve, lambda h: h.tensor_copy(out=gm[:, qb, 0:own], in_=pg[:, qb * 16:qb * 16 + own]), reads=[tpg], writes=[t_gmq[qb]])
                    k.op(k.dve, lambda h: h.max(out=mx[:, qb, :], in_=gm[:, qb, :]), reads=[t_gmq[qb]], writes=[t_mxq[qb]])
                    k.op(k.dve, lambda h: h.tensor_scalar(out=nm[:, qb, 0:own], in0=gm[:, qb, 0:own], scalar1=mx[:, qb, 2:3], scalar2=NEG, op0=ALU.is_lt, op1=ALU.mult),
                         reads=[t_gmq[qb], t_mxq[qb]], writes=[t_nmq[qb]])
                for g4 in range(8):
                    bank = 5 + g4 % 2
                    pn, tpn = H.P[bank], H.t_P[bank]
                    q0 = 2 if g4 == 0 else 4 * g4
                    for qb in range(q0, 4 * g4 + 4):
                        k.op(k.pe, lambda h: h.matmul(pn[0:16, (qb % 4) * 128:(qb % 4 + 1) * 128], nm[:, qb, :], H.ident, start=True, stop=True),
                             reads=[t_nmq[qb], H.t], writes=[tpn])
                    k.op(k.act, lambda h: h.activation(out=negT[:, q0 * 128:(4 * g4 + 4) * 128], in_=pn[0:16, (q0 % 4) * 128:512], func=AF.Copy),
                         reads=[tpn], writes=[t_neg])
                for j in range(8):
                    def parts(kb, q_lo):
                        p = [(kT[s][:, kb * 128:(kb + 1) * 128], qT[s][:, j * 512 + q_lo:(j + 1) * 512], [t_k[s], t_q[s]])]
                        if kb <= 4 * j + 1:
                            n = kb // 2
                            p.append((esel[:, n * 128:(n + 1) * 128], negT[:, j * 512 + q_lo:(j + 1) * 512], [t_esel, t_neg]))
                        return p
                    st = dict(parts=parts, v=lambda kb: (vv[s][:, kb, :], [t_v[s]]), scale=128 ** -0.5, O=3 + 2 * (j % 2), D=4 + 2 * (j % 2))
                    run_causal_tile(k, H, A, j, [st], [0, 1, 2, 7])
                    finish(j, 512 + dl * 128)
            k.wait_all(k.sp, [t_oT])
            k.barrier()
    k.es = None


def make_esel():
    e = np.zeros((16, 2048), np.float32)
    for n in range(16):
        e[n, n * 128:(n + 1) * 128] = 1.0
    return e


def h1_layout(r, cd_w_in, cd_w_uq, cd_w_ukv, q_norm, kv_norm):
    cols = list(range(0, 832))
    for dl in range(4):
        hd = 4 * r + dl
        cols += list(range(832 + 128 * hd, 832 + 128 * hd + 128))
        cols += list(range(832 + 1024 + 128 * hd, 832 + 1024 + 128 * hd + 128))
    for dl in range(4):
        hd = 4 * r + dl
        cols += list(range(832 + 2048 + 128 * hd, 832 + 2048 + 128 * hd + 128))
    w_in = lay_rows(cd_w_in[:, np.array(cols)])
    uq_cols = []
    for hl in range(4):
        hc = 4 * r + hl
        uq_cols += list(range(192 * hc, 192 * hc + 192))
    w_uq = lay_rows(cd_w_uq[:, np.array(uq_cols)])
    kv_cols = []
    for hl in range(4):
        hc = 4 * r + hl
        kv_cols += list(range(256 * hc, 256 * hc + 128))
    for hl in range(4):
        hc = 4 * r + hl
        kv_cols += list(range(256 * hc + 128, 256 * hc + 256))
    w_ukv = lay_rows(cd_w_ukv[:, np.array(kv_cols)])
    ng = np.concatenate([q_norm.reshape(4, 128).T, kv_norm.reshape(2, 128).T], axis=1).astype(np.float32)
    return w_in, w_uq, w_ukv, np.ascontiguousarray(ng)


G_FFN = {(0, 0): 0, (0, 1): 1, (1, 0): 2, (1, 1): 3}
G_MIX = {0: 4, 1: 5}
G_PLE = {0: 6, 1: 7}
G_FINAL = 8
NTOK = 2048


def build_fused():
    nc = bass.Bass("TRN2", target_bir_lowering=False)
    dr = {}

    def dt(name, shape, dty=F32, kind="ExternalInput"):
        dr[name] = nc.dram_tensor(name, list(shape), dty, kind=kind).ap()
    dt("xT", [D, NTOK])
    dt("gains", [128, 9 * NCH])
    dt("sel", [128, 12])
    for l in range(2):
        for i in range(2):
            dt("wgu%d%d" % (l, i), [NHC, 128, NCH, 256])
            dt("wd%d%d" % (l, i), [2, NCH, 128, 22, 128])
        dt("wout%d" % l, [NCH, 128, 12 if l == 0 else 16, 128])
        dt("wgate%d" % l, [NCH, 128, NCH, 128])
        dt("wproj%d" % l, [128, 2, D])
        dt("pT%d" % l, [256, NTOK])
    dt("w_in0", [128, NCH, 3840])
    dt("ropeA", [2, 128, SEQ])
    dt("ropeB", [2, 128, SEQ])
    dt("consts", [128, 1792])
    dt("lamb", [128, 256])
    dt("subln", [128, 1])
    dt("w_in1", [128, NCH, 2368])
    dt("w_uq", [128, 4, 768])
    dt("w_ukv", [128, 2, 1024])
    dt("ng", [128, 6])
    dt("esel", [16, 2048])
    dt("outT", [D, NTOK], F32, "ExternalOutput")
    I = "Internal"
    dt("x1", [D, NTOK], F32, I)
    dt("x2", [D, NTOK], F32, I)
    for i4 in range(4):
        dt("hm%d" % i4, [D, 512], BF16, I)
        dt("hg%d" % i4, [2 * D, 512], BF16, I)
        dt("o1m%d" % i4, [256, SEQ], BF16, I)
        dt("o1g%d" % i4, [512, SEQ], BF16, I)
    for i3 in range(3):
        dt("o0m%d" % i3, [256, SEQ], BF16, I)
        dt("o0g%d" % i3, [512, SEQ], BF16, I)
    dt("o0s", [1536, NTOK], BF16, I)
    dt("o1s", [2048, NTOK], BF16, I)
    dt("qk", [20, 128, SEQ], BF16, I)
    dt("va", [SEQ, 512], BF16, I)
    dt("vb", [3, SEQ, 256], BF16, I)
    for nm_, shp in (("qn", [4, 128, SEQ]), ("qr", [4, 64, SEQ]), ("kn", [4, 128, SEQ]), ("kr", [64, SEQ]), ("vc", [SEQ, 512]),
                     ("qkd", [8, 128, SEQ]), ("vd", [SEQ, 512])):
        dt(nm_, shp, BF16, I)

    def sub(**kw):
        d = dict(dr)
        d.update({a: dr[b] for a, b in kw.items()})
        return d

    with ExitStack() as es:
        k = K(nc, es)
        k.es = es
        sel = k.sb("sel", [128, 12], F32)
        t_sel = Tl("sel")
        k.dma(k.sp, sel[:], dr["sel"], writes=[t_sel])

        def tphase(prefix, steps, d, pre=None):
            with ExitStack() as es2:
                k.es = es2
                k.prefix = prefix
                S = TState(k)
                d = dict(d)
                d["_sel"] = (sel, t_sel)
                t_phase(k, S, NTOK, d, steps, pre=pre)
                k.barrier()

        t_hg = [Tl("hg%d" % i) for i in range(4)]
        dr["_thg"] = t_hg

        def hgather():
            k.pair_gather([dr["hm%d" % i][:, :] for i in range(4)], [dr["hg%d" % i][:, :] for i in range(4)], tiles=t_hg)

        tphase("t1_", [("ffn", "wgu00", "wd00", G_FFN[(0, 0)] * NCH), ("hout", G_MIX[0] * NCH)],
               sub(xT="xT", xT_out="x1"))
        k.prefix = "h0_"
        h0_phase(k, sub(w_in="w_in0"), gather=hgather, opfx="o0m")
        k.barrier()
        tphase("t2_", [("oproj", "wout0", 12, "o0g", 768), ("ffn", "wgu01", "wd01", G_FFN[(0, 1)] * NCH), ("ple", "wgate0", "wproj0", "pT0", G_PLE[0] * NCH),
                       ("ffn", "wgu10", "wd10", G_FFN[(1, 0)] * NCH), ("hout", G_MIX[1] * NCH)],
               sub(xT="x1", xT_out="x2"),
               pre=lambda: k.pair_gather([dr["o0m%d" % i][:, :] for i in range(3)], [dr["o0g%d" % i][:, :] for i in range(3)]))
        k.prefix = "h1_"
        h1_phase(k, sub(w_in="w_in1"), gather=hgather, opfx="o1m")
        k.barrier()
        tphase("t3_", [("oproj", "wout1", 16, "o1g", 1024), ("ffn", "wgu11", "wd11", G_FFN[(1, 1)] * NCH), ("ple", "wgate1", "wproj1", "pT1", G_PLE[1] * NCH),
                       ("final", G_FINAL * NCH)],
               sub(xT="x2", xT_out="outT"),
               pre=lambda: k.pair_gather([dr["o1m%d" % i][:, :] for i in range(4)], [dr["o1g%d" % i][:, :] for i in range(4)]))
    return nc


def kernel(x, p, ffn_norm, ffn_w_gu, ffn_w_down, mix_norm, ab_w_in, ab_lambda, ab_subln, ab_w_out,
           cd_w_in, cd_q_norm, cd_w_uq, cd_kv_norm, cd_w_ukv, cd_w_out, ple_norm, ple_w_gate,
           ple_w_proj, final_norm):
    f = lambda a: np.asarray(a, dtype=np.float32)
    x, p = f(x), f(p)
    ffn_norm, ffn_w_gu, ffn_w_down, mix_norm = f(ffn_norm), f(ffn_w_gu), f(ffn_w_down), f(mix_norm)
    ab_w_in, ab_lambda, ab_subln, ab_w_out = f(ab_w_in), f(ab_lambda), f(ab_subln), f(ab_w_out)
    cd_w_in, cd_q_norm, cd_w_uq, cd_kv_norm, cd_w_ukv, cd_w_out = f(cd_w_in), f(cd_q_norm), f(cd_w_uq), f(cd_kv_norm), f(cd_w_ukv), f(cd_w_out)
    ple_norm, ple_w_gate, ple_w_proj, final_norm = f(ple_norm), f(ple_w_gate), f(ple_w_proj), f(final_norm)

    shared = {
        "gains": lay_gains([ffn_norm[0, 0], ffn_norm[0, 1], ffn_norm[1, 0], ffn_norm[1, 1], mix_norm[0], mix_norm[1],
                            ple_norm[0], ple_norm[1], final_norm]),
        "consts": make_consts(), "ropeA": make_rope(64, 128, 32), "ropeB": make_rope(128, 128, 64), "esel": make_esel(),
        "lamb": np.ascontiguousarray(np.broadcast_to(ab_lambda[0].reshape(1, 256), (128, 256))),
        "subln": np.ascontiguousarray(ab_subln[0].reshape(128, 1)),
        "wout0": lay_wsq(ab_w_out[0]), "wout1": lay_wsq(cd_w_out[0]),
    }
    for l in range(2):
        for i in range(2):
            shared["wgu%d%d" % (l, i)] = lay_wgu(ffn_w_gu[l, i])
            shared["wd%d%d" % (l, i)] = lay_wd(ffn_w_down[l, i])
        shared["wgate%d" % l] = lay_wsq(ple_w_gate[l])
        shared["wproj%d" % l] = lay_rows(ple_w_proj[l])
    w0 = [lay_rows(ab_w_in[0][:, h0_cols(r)]) for r in range(2)]
    lay1 = [h1_layout(r, cd_w_in[0], cd_w_uq[0], cd_w_ukv[0], cd_q_norm[0], cd_kv_norm[0]) for r in range(2)]
    in_maps = []
    for c in range(8):
        b, r = c // 2, c % 2
        ts = slice(r * NTOK, (r + 1) * NTOK)
        m = dict(shared)
        m["xT"] = np.ascontiguousarray(x[b, ts, :].T)
        for l in range(2):
            m["pT%d" % l] = np.ascontiguousarray(p[l, b, ts, :].T)
        sel = np.zeros((128, 12), np.float32)
        sel[:, r] = 1.0
        m["sel"] = sel
        m["w_in0"] = w0[r]
        m["w_in1"], m["w_uq"], m["w_ukv"], m["ng"] = lay1[r]
        in_maps.append(m)
    nc = build_fused()
    res = run_bass_kernel_spmd(nc, in_maps, core_ids=list(range(8))).results
    out = np.empty((4, SEQ, D), np.float32)
    for c in range(8):
        b, r = c // 2, c % 2
        out[b, r * NTOK:(r + 1) * NTOK, :] = res[c]["outT"].T
    return out
```

```python
import math
from contextlib import ExitStack

import numpy as np
import ml_dtypes

import concourse.bass as bass
import concourse.mybir as mybir
from concourse.bass_utils import run_bass_kernel_spmd

F32 = mybir.dt.float32
BF16 = mybir.dt.bfloat16
AF = mybir.ActivationFunctionType
ALU = mybir.AluOpType
NPBF = ml_dtypes.bfloat16

D = 2048
DFF = 5632
NCH = 16
NHC = 44
SEQ = 4096
EPS = 1e-6
NEG = -30000.0


class Tl:
    __slots__ = ("w", "r", "dsem", "dcnt", "name", "sw")

    def __init__(self, name=""):
        self.w = None
        self.r = {}
        self.dsem = None
        self.dcnt = 0
        self.name = name
        self.sw = False


class Eng:
    def __init__(self, name, h, sem):
        self.name = name
        self.h = h
        self.sem = sem
        self.cnt = 0
        self.seen = {}


class K:
    def __init__(self, nc, es):
        self.nc = nc
        self.es = es
        self.sem_es = es
        self.nsem = 0
        mk = lambda n, h: Eng(n, h, es.enter_context(nc.semaphore("s_" + n)))
        self.pe = mk("pe", nc.tensor)
        self.act = mk("act", nc.scalar)
        self.dve = mk("dve", nc.vector)
        self.pool = mk("pool", nc.gpsimd)
        self.sp = mk("sp", nc.sync)
        self.same_engine_sync = True
        self.prefix = ""
        self.dma_tiles = []
        self.free_sems = []
        self.free_sw = []
        self.ncoll = 0

    def barrier(self):
        engs = [self.pe, self.act, self.dve, self.pool, self.sp]
        for e in engs:
            for e2 in engs:
                if e2 is not e and e2.cnt > 0 and e.seen.get(id(e2.sem), 0) < e2.cnt:
                    e.h.wait_ge(e2.sem, e2.cnt)
                    e.seen[id(e2.sem)] = e2.cnt
            for t in self.dma_tiles:
                if e.seen.get(id(t.dsem), 0) < t.dcnt:
                    e.h.wait_ge(t.dsem, t.dcnt)
                    e.seen[id(t.dsem)] = t.dcnt
        for t in self.dma_tiles:
            (self.free_sw if t.sw else self.free_sems).append((t.dsem, t.dcnt))
            t.dsem = None
        self.dma_tiles = []

    def pair_gather(self, srcs, dsts, tiles=None):
        sems = []
        for src, dst in zip(srcs, dsts):
            csem = self.sem_es.enter_context(self.nc.semaphore("c%d" % self.ncoll))
            self.ncoll += 1
            self.nc.gpsimd.collective_compute("AllGather", ALU.bypass, replica_groups=[[0, 1], [2, 3], [4, 5], [6, 7]],
                                              ins=[src], outs=[dst]).then_inc(csem, 1)
            sems.append(csem)
        if tiles is not None:
            for t, csem in zip(tiles, sems):
                t.w = (csem, 1)
                t.r = {}
            return
        for e in [self.pe, self.act, self.dve, self.pool, self.sp]:
            for csem in sems:
                e.h.wait_ge(csem, 1)

    def sb(self, name, shape, dt):
        return self.es.enter_context(self.nc.sbuf_tensor("sb_" + self.prefix + name, shape, dt))

    def ps(self, name, shape, dt=F32):
        return self.es.enter_context(self.nc.psum_tensor("ps_" + self.prefix + name, shape, dt))

    def _waits(self, e, reads, writes, attach=False):
        deps = {}

        def need(tok):
            if tok is None:
                return
            s, c = tok
            if deps.get(id(s), (None, 0))[1] < c:
                deps[id(s)] = (s, c)

        for b in reads:
            need(b.w)
        for b in writes:
            need(b.w)
            for s, c in b.r.items():
                need((s, c))
        todo = []
        for s, c in deps.values():
            if s is e.sem and (e is self.pe or not self.same_engine_sync):
                continue
            if e.seen.get(id(s), 0) < c:
                todo.append((s, c))
                e.seen[id(s)] = c
        last = todo.pop() if (attach and todo) else None
        for s, c in todo:
            e.h.wait_ge(s, c)
        return last

    def op(self, e, fn, reads=(), writes=()):
        last = self._waits(e, reads, writes, attach=True)
        inst = fn(e.h)
        if last is not None:
            inst._wait_ge(last[0], last[1])
        e.cnt += 1
        inst.then_inc(e.sem, 1)
        for b in reads:
            b.r[e.sem] = e.cnt
        for b in writes:
            b.w = (e.sem, e.cnt)
            b.r = {}
        return inst

    def dma(self, q, out, in_, reads=(), writes=(), semt=None):
        if semt is None:
            semt = writes[0] if writes else reads[0]
        sw = q is self.pool
        if semt.dsem is None:
            pool_ = self.free_sw if sw else self.free_sems
            semt.sw = sw
            if pool_:
                semt.dsem, semt.dcnt = pool_.pop()
            else:
                semt.dsem = self.sem_es.enter_context(self.nc.semaphore("d%d" % self.nsem))
                self.nsem += 1
            self.dma_tiles.append(semt)
        self._waits(q, reads, writes)
        inst = q.h.dma_start(out=out, in_=in_)
        semt.dcnt += 16
        inst.then_inc(semt.dsem, 16)
        for b in reads:
            b.r[semt.dsem] = semt.dcnt
        for b in writes:
            b.w = (semt.dsem, semt.dcnt)
            b.r = {}
        return inst

    def wait_all(self, e, tiles):
        self._waits(e, tiles, tiles)


TT = 1024
NTH = TT // 512


class TState:
    def __init__(self, k):
        self.k = k
        self.xs = k.sb("xs", [128, NCH, TT], F32)
        self.hT = k.sb("hT", [128, NCH, TT], BF16)
        self.aT = k.sb("aT", [128, 22, TT], BF16)
        self.sq = k.sb("sq", [128, NCH, 512], BF16)
        self.rstd = k.sb("rstd", [128, 512], F32)
        self.sg = k.sb("sg", [128, 2, 512], F32)
        self.wgu = [k.sb("wgu%d" % i, [128, NCH, 256], BF16) for i in range(2)]
        self.wd = [k.sb("wd%d" % i, [128, 22, 128], BF16) for i in range(3)]
        self.pTs = k.sb("pTs", [128, 2, TT], BF16)
        self.wpj = k.sb("wpj", [128, 2, D], BF16)
        self.ones = k.sb("ones", [128, 128], BF16)
        self.gains = k.sb("gains", [128, 9 * NCH], F32)
        self.P = [k.ps("P%d" % i, [128, 512]) for i in range(8)]
        T = Tl
        self.t_xs = [[T("xs%d_%d" % (i, fc)) for fc in range(NCH)] for i in range(NTH)]
        self.t_hT = [T("hT%d" % i) for i in range(NTH)]
        self.t_aT = [T("aT%d" % i) for i in range(NTH)]
        self.t_sq = T("sq")
        self.t_sqf = [T("sq%d" % fc) for fc in range(NCH)]
        self.t_rstd = T("rstd")
        self.t_sg = [T("sg0"), T("sg1")]
        self.t_wgu = [T("wgu0"), T("wgu1")]
        self.t_wd = [T("wd0"), T("wd1"), T("wd2")]
        self.t_pT = T("pT")
        self.t_wpj = T("wpj")
        self.t_ones = T("ones")
        self.t_gains = T("gains")
        self.t_P = [T("P%d" % i) for i in range(8)]
        self.iwgu = 0
        self.iwd = 0
        self.igu = 0
        self.idn = 0
        k.op(k.pool, lambda h: h.memset(self.ones[:], 1.0), writes=[self.t_ones])


def t_norm(S, gcol, out=None, t_out=None):
    k = S.k
    for th in range(NTH):
        cs = slice(th * 512, (th + 1) * 512)
        for fc in range(NCH):
            k.op(k.act, lambda h, fc=fc: h.activation(out=S.sq[:, fc, :], in_=S.xs[:, fc, cs], func=AF.Square),
                 reads=[S.t_xs[th][fc]], writes=[S.t_sqf[fc]])
        pss, tpss = S.P[6 + th % 2], S.t_P[6 + th % 2]
        for fc in range(NCH):
            k.op(k.pe, lambda h, fc=fc: h.matmul(pss[:], S.ones[:], S.sq[:, fc, :], start=(fc == 0), stop=(fc == NCH - 1)),
                 reads=[S.t_sqf[fc], S.t_ones], writes=[tpss])
        k.op(k.act, lambda h: h.activation(out=S.rstd[:], in_=pss[:], func=AF.Sqrt, scale=1.0 / D, bias=EPS),
             reads=[tpss], writes=[S.t_rstd])
        k.op(k.dve, lambda h: h.reciprocal(out=S.rstd[:], in_=S.rstd[:]), reads=[S.t_rstd], writes=[S.t_rstd])
        for fc in range(NCH):
            if out is None:
                dst, tdst = S.hT[:, fc, cs], S.t_hT[th]
            else:
                dst, tdst = out[:, fc, cs], t_out[th][fc]
            k.op(k.dve, lambda h, fc=fc, dst=dst: h.scalar_tensor_tensor(
                out=dst, in0=S.xs[:, fc, cs], scalar=S.gains[:, gcol + fc:gcol + fc + 1], in1=S.rstd[:],
                op0=ALU.mult, op1=ALU.mult), reads=[S.t_xs[th][fc], S.t_rstd, S.t_gains], writes=[tdst])


def t_ffn(S, wgu_d, wd_d, gcol):
    k = S.k
    t_norm(S, gcol)
    for half in range(2):
        def load_gu(c):
            i = S.iwgu % 2
            S.iwgu += 1
            k.dma(k.pool, S.wgu[i][:], wgu_d[half * 22 + c], writes=[S.t_wgu[i]])
            return i
        slot = load_gu(0)
        for c in range(22):
            nslot = load_gu(c + 1) if c + 1 < 22 else None
            w, tw = S.wgu[slot], S.t_wgu[slot]
            for th in range(NTH):
                cs = slice(th * 512, (th + 1) * 512)
                u = S.igu % 2
                S.igu += 1
                pg, pu, tpg, tpu = S.P[2 * u], S.P[2 * u + 1], S.t_P[2 * u], S.t_P[2 * u + 1]
                for kc in range(NCH):
                    k.op(k.pe, lambda h, kc=kc: h.matmul(pg[:], w[:, kc, 0:128], S.hT[:, kc, cs], start=(kc == 0), stop=(kc == NCH - 1)),
                         reads=[tw, S.t_hT[th]], writes=[tpg])
                for kc in range(NCH):
                    k.op(k.pe, lambda h, kc=kc: h.matmul(pu[:], w[:, kc, 128:256], S.hT[:, kc, cs], start=(kc == 0), stop=(kc == NCH - 1)),
                         reads=[tw, S.t_hT[th]], writes=[tpu])
                sg, tsg = S.sg[:, u, :], S.t_sg[u]
                k.op(k.act, lambda h: h.activation(out=sg, in_=pg[:], func=AF.Silu), reads=[tpg], writes=[tsg])
                k.op(k.dve, lambda h, c=c: h.tensor_tensor(out=S.aT[:, c, cs], in0=sg, in1=pu[:], op=ALU.mult),
                     reads=[tsg, tpu], writes=[S.t_aT[th]])
            slot = nslot
        def load_d(oc):
            i = S.iwd % 3
            S.iwd += 1
            k.dma(k.pool, S.wd[i][:], wd_d[half, oc], writes=[S.t_wd[i]])
            return i
        slots = [load_d(0), load_d(1)]
        for oc in range(NCH):
            if oc + 2 < NCH:
                slots.append(load_d(oc + 2))
            w, tw = S.wd[slots[oc]], S.t_wd[slots[oc]]
            for th in range(NTH):
                cs = slice(th * 512, (th + 1) * 512)
                u = S.idn % 2
                S.idn += 1
                pd, tpd = S.P[4 + u], S.t_P[4 + u]
                for kc in range(22):
                    k.op(k.pe, lambda h, kc=kc: h.matmul(pd[:], w[:, kc, :], S.aT[:, kc, cs], start=(kc == 0), stop=(kc == 21)),
                         reads=[tw, S.t_aT[th]], writes=[tpd])
                k.op(k.dve, lambda h, oc=oc: h.scalar_tensor_tensor(
                    out=S.xs[:, oc, cs], in0=pd[:], scalar=0.5, in1=S.xs[:, oc, cs], op0=ALU.mult, op1=ALU.add),
                    reads=[tpd, S.t_xs[th][oc]], writes=[S.t_xs[th][oc]])


def t_proj_add(S, w_d, nkc, src, t_src, mode, gate_w_d=None, pj=None):
    k = S.k

    def load(oc):
        i = S.iwd % 3
        S.iwd += 1
        k.dma(k.pool, S.wd[i][:, 0:nkc, :], w_d[oc], writes=[S.t_wd[i]])
        return i
    slots = [load(0), load(1)]
    for oc in range(NCH):
        if oc + 2 < NCH:
            slots.append(load(oc + 2))
        w, tw = S.wd[slots[oc]], S.t_wd[slots[oc]]
        for th in range(NTH):
            cs = slice(th * 512, (th + 1) * 512)
            u = S.idn % 2
            S.idn += 1
            pd, tpd = S.P[4 + u], S.t_P[4 + u]
            for kc in range(nkc):
                k.op(k.pe, lambda h, kc=kc: h.matmul(pd[:], w[:, kc, :], src[:, kc, cs], start=(kc == 0), stop=(kc == nkc - 1)),
                     reads=[tw, t_src[th]], writes=[tpd])
            if mode == "add":
                k.op(k.dve, lambda h, oc=oc: h.tensor_tensor(out=S.xs[:, oc, cs], in0=pd[:], in1=S.xs[:, oc, cs], op=ALU.add),
                     reads=[tpd, S.t_xs[th][oc]], writes=[S.t_xs[th][oc]])
            else:
                pq, tpq = S.P[6 + u], S.t_P[6 + u]
                for kc in range(2):
                    k.op(k.pe, lambda h, kc=kc, oc=oc: h.matmul(pq[:], S.wpj[:, kc, oc * 128:(oc + 1) * 128], S.pTs[:, kc, cs],
                                                                start=(kc == 0), stop=(kc == 1)),
                         reads=[S.t_wpj, S.t_pT], writes=[tpq])
                sg, tsg = S.sg[:, u, :], S.t_sg[u]
                k.op(k.act, lambda h: h.activation(out=sg, in_=pd[:], func=AF.Sigmoid), reads=[tpd], writes=[tsg])
                k.op(k.dve, lambda h: h.tensor_tensor(out=sg, in0=sg, in1=pq[:], op=ALU.mult), reads=[tsg, tpq], writes=[tsg])
                k.op(k.dve, lambda h, oc=oc: h.tensor_tensor(out=S.xs[:, oc, cs], in0=sg, in1=S.xs[:, oc, cs], op=ALU.add),
                     reads=[tsg, S.t_xs[th][oc]], writes=[S.t_xs[th][oc]])


def t_phase(k, S, ntok, dr, steps, pre=None):
    t_x = Tl("xT_d")
    t_h = Tl("hT_d")
    t_out = Tl("out_d")
    k.dma(k.sp, S.gains[:], dr["gains"], writes=[S.t_gains])
    for tt in range(ntok // TT):
        ts = slice(tt * TT, (tt + 1) * TT)
        for th in range(NTH):
            k.dma(k.sp, S.xs[:, :, th * 512:(th + 1) * 512],
                  dr["xT"].rearrange("(c p) t -> p c t", p=128)[:, :, tt * TT + th * 512: tt * TT + (th + 1) * 512],
                  writes=S.t_xs[th])
        if tt == 0 and pre is not None:
            pre()
        for st in steps:
            kind = st[0]
            if kind == "ffn":
                t_ffn(S, dr[st[1]], dr[st[2]], st[3])
            elif kind == "oproj":
                nkc, gpfx, rows_half = st[2], st[3], st[4]
                sel, t_sel = dr["_sel"]
                nh = rows_half // 128
                na = 4
                nb = nh - na
                for th in range(NTH):
                    cs = slice(th * 512, (th + 1) * 512)
                    for kc in range(nkc):
                        if kc < na:
                            rr, lc = 0, kc
                        elif kc < 2 * na:
                            rr, lc = 1, kc - na
                        elif kc < 2 * na + nb:
                            rr, lc = 0, na + kc - 2 * na
                        else:
                            rr, lc = 1, na + kc - 2 * na - nb
                        og = dr[gpfx + "%d" % (lc // 2)]
                        r0 = rr * 256 + (lc % 2) * 128
                        c0 = tt * TT + th * 512
                        k.dma(k.sp, S.aT[:, kc, cs], og[r0:r0 + 128, c0:c0 + 512], writes=[S.t_aT[th]])
                        k.dma(k.sp, S.sq[:, kc, :], og[r0:r0 + 128, NTOK + c0:NTOK + c0 + 512], writes=[S.t_sqf[kc]])
                    k.op(k.dve, lambda h: h.tensor_scalar(out=S.hT[:, 0:nkc, cs], in0=S.aT[:, 0:nkc, cs], scalar1=sel[:, 0:1], scalar2=None, op0=ALU.mult),
                         reads=[S.t_aT[th], t_sel], writes=[S.t_hT[th]])
                    k.op(k.dve, lambda h: h.scalar_tensor_tensor(out=S.hT[:, 0:nkc, cs], in0=S.sq[:, 0:nkc, :], scalar=sel[:, 1:2], in1=S.hT[:, 0:nkc, cs],
                                                                 op0=ALU.mult, op1=ALU.add),
                         reads=S.t_sqf[0:nkc] + [S.t_hT[th], t_sel], writes=[S.t_hT[th]])
                t_proj_add(S, dr[st[1]], nkc, S.hT, S.t_hT, "add")
            elif kind == "ple":
                k.dma(k.pool, S.wpj[:], dr[st[2]], writes=[S.t_wpj])
                k.dma(k.pool, S.pTs[:], dr[st[3]].rearrange("(c p) t -> p c t", p=128)[:, :, ts], writes=[S.t_pT])
                t_norm(S, st[4])
                t_proj_add(S, dr[st[1]], NCH, S.hT, S.t_hT, "ple")
            elif kind == "hout":
                t_norm(S, st[1])
                for th in range(NTH):
                    k.dma(k.sp, dr["hm%d" % (tt * NTH + th)].rearrange("(c p) t -> p c t", p=128),
                          S.hT[:, :, th * 512:(th + 1) * 512], reads=[S.t_hT[th]], writes=[t_h], semt=t_h)
            elif kind == "final":
                t_norm(S, st[1], out=S.xs, t_out=S.t_xs)
        for th in range(NTH):
            k.dma(k.sp, dr["xT_out"].rearrange("(c p) t -> p c t", p=128)[:, :, tt * TT + th * 512: tt * TT + (th + 1) * 512],
                  S.xs[:, :, th * 512:(th + 1) * 512], reads=S.t_xs[th], writes=[t_out], semt=t_out)
    k.wait_all(k.sp, [t_out, t_h])


def lay_wgu(w):
    g = w[:, :DFF].reshape(NCH, 128, NHC, 128)
    u = w[:, DFF:].reshape(NCH, 128, NHC, 128)
    out = np.empty((NHC, 128, NCH, 256), np.float32)
    out[:, :, :, :128] = g.transpose(2, 1, 0, 3)
    out[:, :, :, 128:] = u.transpose(2, 1, 0, 3)
    return out


def lay_wd(w):
    a = w.reshape(2, 22, 128, NCH, 128)
    return np.ascontiguousarray(a.transpose(0, 3, 2, 1, 4))


def lay_wsq(w):
    nkc = w.shape[0] // 128
    a = w.reshape(nkc, 128, NCH, 128)
    return np.ascontiguousarray(a.transpose(2, 1, 0, 3))


def lay_rows(w):
    nkc = w.shape[0] // 128
    return np.ascontiguousarray(w.reshape(nkc, 128, w.shape[1]).transpose(1, 0, 2))


def lay_gains(vs):
    return np.ascontiguousarray(np.concatenate([v.reshape(NCH, 128).T for v in vs], axis=1)).astype(np.float32)


class HConst:
    def __init__(self, k, consts_d):
        self.k = k
        self.c = k.sb("hconst", [128, 1792 + 2048], BF16)
        self.t = Tl("hconst")
        k.dma(k.pool, self.c[:, 0:1792], consts_d, writes=[self.t])
        self.ident = self.c[:, 0:128]
        self.permA = self.c[:, 128:256]
        self.permB = self.c[:, 256:384]
        self.tri = self.c[:, 384:512]
        self.mask4 = self.c[:, 512:1024]
        self.mask4f = self.c[:, 1024:1536]
        self.ones = self.c[:, 1792:1920]
        k.op(k.pool, lambda h: h.memset(self.c[:, 1792:1920], 1.0), writes=[self.t])
        self.ones32 = k.sb("ones32", [128, 128], F32)
        k.op(k.pool, lambda h: h.memset(self.ones32[:], 1.0), writes=[self.t])
        self.P = [k.ps("HP%d" % i, [128, 512]) for i in range(8)]
        self.t_P = [Tl("HP%d" % i) for i in range(8)]

    def esel(self, n):
        return self.c[0:16, 1536 + n * 16:1536 + n * 16 + 16]


def make_consts():
    c = np.zeros((128, 1792), np.float32)
    idx = np.arange(128)
    c[idx, idx] = 1.0
    c[idx ^ 32, 128 + idx] = 1.0
    c[idx ^ 64, 256 + idx] = 1.0
    kk, qq = np.meshgrid(idx, idx, indexing="ij")
    le = (kk <= qq).astype(np.float32)
    ge = (kk >= qq).astype(np.float32)
    c[:, 384:512] = le
    c[:, 512:1024] = np.concatenate([ge, le, ge, le], axis=1)
    c[:, 1024:1536] = np.concatenate([np.zeros_like(ge), le, ge, le], axis=1)
    return c


def make_rope(dim, layout_rows, half):
    inv = (1.0 / (10000.0 ** (np.arange(0, dim, 2, dtype=np.float32) / np.float32(dim)))).astype(np.float32)
    ang = (np.arange(SEQ, dtype=np.float32)[:, None] * inv[None, :]).astype(np.float32)
    cos = np.cos(ang).astype(np.float32).T
    sin = np.sin(ang).astype(np.float32).T
    m = np.arange(layout_rows)
    j = m % half
    sign = np.where((m % (2 * half)) < half, -1.0, 1.0).astype(np.float32)
    return np.ascontiguousarray(np.stack([cos[j], sin[j] * sign[:, None]]))


class RopeUnit:
    def __init__(self, k, H):
        self.k, self.H = k, H
        self.xb = [k.sb("r_xb%d" % i, [128, 512], BF16) for i in range(2)]
        self.t1 = [k.sb("r_t1%d" % i, [128, 512], F32) for i in range(2)]
        self.t2 = [k.sb("r_t2%d" % i, [128, 512], F32) for i in range(2)]
        self.ost = [k.sb("r_os%d" % i, [128, 512], BF16) for i in range(3)]
        self.t_xb = [Tl() for _ in range(2)]
        self.t_t1 = [Tl() for _ in range(2)]
        self.t_t2 = [Tl() for _ in range(2)]
        self.t_ost = [Tl() for _ in range(3)]
        self.i = 0
        self.io = 0

    def store(self, ps, tps, rows, dst, t_dst, perm=None, C=None, S=None, t_tab=None):
        k, H = self.k, self.H
        o = self.io % 3
        self.io += 1
        ost, tost = self.ost[o], self.t_ost[o]
        if perm is None:
            k.op(k.act, lambda h: h.activation(out=ost[0:rows, :], in_=ps[0:rows, :], func=AF.Copy), reads=[tps], writes=[tost])

            def stage_b():
                k.dma(k.sp, dst, ost[0:rows, :], reads=[tost], writes=[t_dst], semt=t_dst)
        else:
            i = self.i % 2
            self.i += 1
            xb, t1, t2 = self.xb[i], self.t1[i], self.t2[i]
            k.op(k.act, lambda h: h.activation(out=xb[0:rows, :], in_=ps[0:rows, :], func=AF.Copy), reads=[tps], writes=[self.t_xb[i]])

            def stage_b():
                pw, tpw = H.P[6 + i], H.t_P[6 + i]
                k.op(k.pe, lambda h: h.matmul(pw[0:rows, :], perm[0:rows, 0:rows], xb[0:rows, :], start=True, stop=True),
                     reads=[self.t_xb[i], H.t], writes=[tpw])
                k.op(k.dve, lambda h: h.tensor_tensor(out=t1[0:rows, :], in0=xb[0:rows, :], in1=C, op=ALU.mult),
                     reads=[self.t_xb[i], t_tab], writes=[self.t_t1[i]])
                k.op(k.dve, lambda h: h.tensor_tensor(out=t2[0:rows, :], in0=pw[0:rows, :], in1=S, op=ALU.mult),
                     reads=[tpw, t_tab], writes=[self.t_t2[i]])
                k.op(k.pool, lambda h: h.tensor_tensor(out=ost[0:rows, :], in0=t1[0:rows, :], in1=t2[0:rows, :], op=ALU.add),
                     reads=[self.t_t1[i], self.t_t2[i]], writes=[tost])
                k.dma(k.sp, dst, ost[0:rows, :], reads=[tost], writes=[t_dst], semt=t_dst)
        self.flush()
        self.pending = stage_b

    def flush(self):
        p = getattr(self, "pending", None)
        self.pending = None
        if p is not None:
            p()


class AttnBufs:
    def __init__(self, k, n=3):
        self.pT = [k.sb("a_pT%d" % i, [128, 512], BF16) for i in range(n)]
        self.t_pT = [Tl() for _ in range(n)]
        self.n = n
        self.i = 0
        self.cnt = 0
        self.acc = [[[k.sb("a_acc%d%d%d" % (s_, e_, p_), [128, 512], F32) for p_ in range(2)] for e_ in range(2)] for s_ in range(2)]
        self.t_acc = [[[Tl() for p_ in range(2)] for e_ in range(2)] for s_ in range(2)]


def run_causal_tile(k, H, A, j, streams, sidx):
    nkb = 4 * j + 4
    steps = [(kb, si) for kb in range(nkb) for si in range(len(streams))]
    two = len(streams) == 2
    for si in range(len(streams)):
        k.op(k.pool, lambda h, si=si: h.memset(A.acc[si][0][j % 2][:], 0.0), writes=[A.t_acc[si][0][j % 2]])
    G = 2
    groups = [steps[i:i + G] for i in range(0, len(steps), G)]
    pend = None
    for gi in range(len(groups) + 1):
        cur = None
        if gi < len(groups):
            cur = []
            plist = []
            for (kb, si) in groups[gi]:
                st = streams[si]
                r = kb - 4 * j
                q_lo = max(0, r) * 128
                sp_i = sidx[A.cnt % len(sidx)]
                A.cnt += 1
                sp, tsp = H.P[sp_i], H.t_P[sp_i]
                plist.append((st["parts"](kb, q_lo), sp, tsp, q_lo))
                cur.append((kb, si, sp, tsp, q_lo, r))
            for pi in range(max(len(p[0]) for p in plist)):
                for (parts, sp, tsp, q_lo) in plist:
                    if pi < len(parts):
                        lhsT, rhs, rd = parts[pi]
                        k.op(k.pe, lambda h: h.matmul(sp[:, q_lo:512], lhsT, rhs, start=(pi == 0), stop=(pi == len(parts) - 1)),
                             reads=rd, writes=[tsp])
        if pend is not None:
            done = []
            for (pkb, psi, psp, ptsp, pq_lo, pr) in pend:
                st = streams[psi]
                b = A.i % A.n
                A.i += 1
                pT, tpT = A.pT[b], A.t_pT[b]
                k.op(k.act, lambda h: h.activation(out=pT[:, pq_lo:512], in_=psp[:, pq_lo:512], func=AF.Exp, scale=st["scale"]),
                     reads=[ptsp], writes=[tpT])
                if pr >= 0:
                    k.op(k.pool, lambda h: h.tensor_tensor(out=pT[:, pq_lo:pq_lo + 128], in0=pT[:, pq_lo:pq_lo + 128], in1=H.tri, op=ALU.mult),
                         reads=[tpT, H.t], writes=[tpT])
                done.append((pkb, psi, pq_lo, pT, tpT))
            for (pkb, psi, pq_lo, pT, tpT) in done:
                st = streams[psi]
                vl, vrd = st["v"](pkb)
                last = (pkb == nkb - 1)
                k.op(k.pe, lambda h: h.matmul(H.P[st["O"]][:, pq_lo:512], vl, pT[:, pq_lo:512], start=(pkb == 0), stop=last),
                     reads=[tpT] + vrd, writes=[H.t_P[st["O"]]])
            for (pkb, psi, pq_lo, pT, tpT) in done:
                st = streams[psi]
                if two and pkb % 3 == 0:
                    k.op(k.pe, lambda h: h.matmul(H.P[st["D"]][:, pq_lo:512], H.ones, pT[:, pq_lo:512], start=(pkb == 0), stop=False),
                         reads=[tpT, H.t], writes=[H.t_P[st["D"]]])
                else:
                    ac, tac = A.acc[psi][0][j % 2], A.t_acc[psi][0][j % 2]
                    k.op(k.dve, lambda h: h.tensor_tensor(out=ac[:, pq_lo:512], in0=ac[:, pq_lo:512], in1=pT[:, pq_lo:512], op=ALU.add),
                         reads=[tpT, tac], writes=[tac])
        pend = cur
    for si, st in enumerate(streams):
        k.op(k.pe, lambda h: h.matmul(H.P[st["D"]][:], H.ones32[:], A.acc[si][0][j % 2][:], start=(not two), stop=True),
             reads=[A.t_acc[si][0][j % 2], H.t], writes=[H.t_P[st["D"]]])


class WTiles:
    def __init__(self, k, w_sb, w_d, ncols, step):
        self.step = step
        self.tl = []
        for c0 in range(0, ncols, step):
            c1 = min(ncols, c0 + step)
            t = Tl()
            k.dma(k.pool, w_sb[:, :, c0:c1], w_d[:, :, c0:c1], writes=[t])
            self.tl.append(t)

    def t(self, c0, n):
        return self.tl[c0 // self.step:(c0 + n - 1) // self.step + 1]


def h0_phase(k, dr, gather=None, opfx="o0m"):
    nc = k.nc
    t_qk = [Tl("qk%d" % i) for i in range(20)]
    t_va = Tl("va")
    t_vb = Tl("vb")
    t_oT = Tl("oT")
    def hload(dst, tdst, tt):
        rk, i4 = tt // 4, tt % 4
        src = dr["hg%d" % i4][rk * D:(rk + 1) * D, :].rearrange("(c p) t -> p c t", p=128)
        k.dma(k.sp, dst[:], src, reads=[dr["_thg"][i4]], writes=[tdst])

    def odst(lc, j):
        return dr[opfx + "%d" % (lc // 2)][(lc % 2) * 128:(lc % 2 + 1) * 128, j * 512:(j + 1) * 512]
    with ExitStack() as es:
        k.es = es
        H = HConst(k, dr["consts"])
        with ExitStack() as es2:
            k.es = es2
            W = k.sb("h0w", [128, NCH, 3840], BF16)
            WT = WTiles(k, W, dr["w_in"], 3840, 640)
            if gather is not None:
                gather()
            hs = [k.sb("h0h%d" % i, [128, NCH, 512], BF16) for i in range(2)]
            t_hs = [Tl(), Tl()]
            tab = [k.sb("h0tab%d" % i, [128, 4, 512], F32) for i in range(2)]
            t_tab = [Tl(), Tl()]
            vst = [k.sb("h0vst%d" % i, [128, 512], BF16) for i in range(2)]
            t_vst = [Tl(), Tl()]
            R = RopeUnit(k, H)
            ivs = 0
            for tt in range(8):
                i = tt % 2
                cs = slice(tt * 512, (tt + 1) * 512)
                hload(hs[i], t_hs[i], tt)
                k.dma(k.sp, tab[i][:, 0:2, :], dr["ropeA"].rearrange("a p t -> p a t")[:, :, cs], writes=[t_tab[i]])
                k.dma(k.sp, tab[i][:, 2:4, :], dr["ropeB"].rearrange("a p t -> p a t")[:, :, cs], writes=[t_tab[i]])
                for gi in range(20):
                    ps, tps = H.P[gi % 2], H.t_P[gi % 2]
                    for kc in range(NCH):
                        k.op(k.pe, lambda h, kc=kc: h.matmul(ps[:], W[:, kc, gi * 128:(gi + 1) * 128], hs[i][:, kc, :],
                                                             start=(kc == 0), stop=(kc == NCH - 1)), reads=WT.t(gi * 128, 128) + [t_hs[i]], writes=[tps])
                    if gi < 8:
                        R.store(ps, tps, 128, dr["qk"][gi][:, cs], t_qk[gi], H.permA, tab[i][:, 0, :], tab[i][:, 1, :], t_tab[i])
                    else:
                        R.store(ps, tps, 128, dr["qk"][gi][:, cs], t_qk[gi], H.permB, tab[i][:, 2, :], tab[i][:, 3, :], t_tab[i])
                for tb in range(4):
                    for (c0, ncol, dst, tdst) in ((2560, 512, dr["va"], t_va), (3072, 512, None, t_vb), (3584, 256, None, t_vb)):
                        ps, tps = H.P[2 + ivs % 2], H.t_P[2 + ivs % 2]
                        for kc in range(NCH):
                            k.op(k.pe, lambda h, kc=kc: h.matmul(ps[:, 0:ncol], hs[i][:, kc, tb * 128:(tb + 1) * 128], W[:, kc, c0:c0 + ncol],
                                                                 start=(kc == 0), stop=(kc == NCH - 1)), reads=WT.t(c0, ncol) + [t_hs[i]], writes=[tps])
                        v, tv = vst[ivs % 2], t_vst[ivs % 2]
                        ivs += 1
                        k.op(k.dve, lambda h: h.tensor_copy(out=v[:, 0:ncol], in_=ps[:, 0:ncol]), reads=[tps], writes=[tv])
                        rows = slice(tt * 512 + tb * 128, tt * 512 + (tb + 1) * 128)
                        if dst is not None:
                            k.dma(k.sp, dst[rows, :], v[:, 0:512], reads=[tv], writes=[tdst], semt=tdst)
                        elif ncol == 512:
                            k.dma(k.sp, dr["vb"][0][rows, :], v[:, 0:256], reads=[tv], writes=[tdst], semt=tdst)
                            k.dma(k.sp, dr["vb"][1][rows, :], v[:, 256:512], reads=[tv], writes=[tdst], semt=tdst)
                        else:
                            k.dma(k.sp, dr["vb"][2][rows, :], v[:, 0:256], reads=[tv], writes=[tdst], semt=tdst)
            R.flush()
            k.barrier()
        with ExitStack() as es2:
            k.es = es2
            A = AttnBufs(k, 6)
            qT = [k.sb("h0q%d" % i, [128, SEQ], BF16) for i in range(2)]
            kT = [k.sb("h0k%d" % i, [128, SEQ], BF16) for i in range(2)]
            vv = [k.sb("h0v%d" % i, [128, 32, 128], BF16) for i in range(2)]
            t_q = [Tl(), Tl()]
            t_k = [Tl(), Tl()]
            t_v = [Tl(), Tl()]
            lam = k.sb("h0lam", [128, 256], F32)
            lam2 = k.sb("h0lam2", [128, 8], F32)
            sub = k.sb("h0sub", [128, 1], F32)
            t_lam = Tl()
            k.dma(k.sp, lam[:], dr["lamb"], writes=[t_lam])
            k.dma(k.sp, sub[:], dr["subln"], writes=[t_lam])
            k.op(k.dve, lambda h: h.tensor_tensor(out=lam[:, 0:64], in0=lam[:, 0:64], in1=lam[:, 64:128], op=ALU.mult), reads=[t_lam], writes=[t_lam])
            k.op(k.dve, lambda h: h.tensor_tensor(out=lam[:, 128:192], in0=lam[:, 128:192], in1=lam[:, 192:256], op=ALU.mult), reads=[t_lam], writes=[t_lam])
            k.op(k.dve, lambda h: h.reduce_sum(out=lam2[:, 0:1], in_=lam[:, 0:64], axis=mybir.AxisListType.X), reads=[t_lam], writes=[t_lam])
            k.op(k.dve, lambda h: h.reduce_sum(out=lam2[:, 1:2], in_=lam[:, 128:192], axis=mybir.AxisListType.X), reads=[t_lam], writes=[t_lam])
            k.op(k.act, lambda h: h.activation(out=lam2[:, 2:4], in_=lam2[:, 0:2], func=AF.Exp), reads=[t_lam], writes=[t_lam])
            k.op(k.dve, lambda h: h.tensor_tensor(out=lam2[:, 4:5], in0=lam2[:, 3:4], in1=lam2[:, 2:3], op=ALU.subtract), reads=[t_lam], writes=[t_lam])
            k.op(k.dve, lambda h: h.tensor_scalar(out=lam2[:, 4:5], in0=lam2[:, 4:5], scalar1=-0.2, scalar2=1.0, op0=ALU.add, op1=ALU.mult), reads=[t_lam], writes=[t_lam])
            k.op(k.dve, lambda h: h.tensor_scalar(out=sub[:], in0=sub[:], scalar1=0.8, scalar2=0.0, op0=ALU.mult, op1=ALU.add), reads=[t_lam], writes=[t_lam])
            neglam = lam2[:, 4:5]
            rc = [k.sb("h0rc%d" % i, [128, 512], F32) for i in range(2)]
            oo = [k.sb("h0oo%d" % i, [128, 512], F32) for i in range(2)]
            sqb = k.sb("h0sq", [128, 512], BF16)
            ost = [k.sb("h0ost%d" % i, [128, 512], BF16) for i in range(2)]
            t_rc = [Tl(), Tl()]
            t_oo = [Tl(), Tl()]
            t_sq = Tl()
            t_ost = [Tl(), Tl()]
            io = 0

            def load_head(slot, gq, gk, vsrc):
                k.dma(k.sp, qT[slot][:], dr["qk"][gq], reads=[t_qk[gq]], writes=[t_q[slot]])
                k.dma(k.sp, kT[slot][:], dr["qk"][gk], reads=[t_qk[gk]], writes=[t_k[slot]])
                k.dma(k.sp, vv[slot][:], vsrc, reads=[t_va, t_vb], writes=[t_v[slot]])

            va_v = dr["va"].rearrange("(b p) c -> p b c", p=128)
            load_head(0, 0, 1, va_v[:, :, 0:128])
            for al in range(4):
                s = al % 2
                if al + 1 < 4:
                    load_head((al + 1) % 2, 2 * (al + 1), 2 * (al + 1) + 1, va_v[:, :, (al + 1) * 128:(al + 2) * 128])
                for j in range(8):
                    streams = []
                    for m in range(2):
                        rows = slice(m * 64, (m + 1) * 64)
                        streams.append(dict(
                            parts=lambda kb, q_lo, rows=rows: [(kT[s][rows, kb * 128:(kb + 1) * 128], qT[s][rows, j * 512 + q_lo:(j + 1) * 512], [t_k[s], t_q[s]])],
                            v=lambda kb: (vv[s][:, kb, :], [t_v[s]]), scale=0.125, O=3 + m, D=5 + m))
                    run_causal_tile(k, H, A, j, streams, [0, 1, 2, 7])
                    for m in range(2):
                        k.op(k.dve, lambda h, m=m: h.reciprocal(out=rc[m][:], in_=H.P[5 + m][:]), reads=[H.t_P[5 + m]], writes=[t_rc[m]])
                        k.op(k.dve, lambda h, m=m: h.tensor_tensor(out=oo[m][:], in0=H.P[3 + m][:], in1=rc[m][:], op=ALU.mult),
                             reads=[H.t_P[3 + m], t_rc[m]], writes=[t_oo[m]])
                    k.op(k.dve, lambda h: h.scalar_tensor_tensor(out=oo[0][:], in0=oo[1][:], scalar=neglam, in1=oo[0][:], op0=ALU.mult, op1=ALU.add),
                         reads=[t_oo[0], t_oo[1], t_lam], writes=[t_oo[0]])
                    k.op(k.act, lambda h: h.activation(out=sqb[:], in_=oo[0][:], func=AF.Square), reads=[t_oo[0]], writes=[t_sq])
                    k.op(k.pe, lambda h: h.matmul(H.P[5][:], H.ones, sqb[:], start=True, stop=True), reads=[t_sq, H.t], writes=[H.t_P[5]])
                    k.op(k.act, lambda h: h.activation(out=rc[0][:], in_=H.P[5][:], func=AF.Sqrt, scale=1.0 / 128, bias=EPS),
                         reads=[H.t_P[5]], writes=[t_rc[0]])
                    k.op(k.dve, lambda h: h.reciprocal(out=rc[0][:], in_=rc[0][:]), reads=[t_rc[0]], writes=[t_rc[0]])
                    o_ = ost[io % 2]
                    to_ = t_ost[io % 2]
                    io += 1
                    k.op(k.dve, lambda h: h.scalar_tensor_tensor(out=o_[:], in0=oo[0][:], scalar=sub[:, 0:1], in1=rc[0][:], op0=ALU.mult, op1=ALU.mult),
                         reads=[t_oo[0], t_rc[0], t_lam], writes=[to_])
                    k.dma(k.sp, odst(al, j), o_[:], reads=[to_], writes=[t_oT], semt=t_oT)
            num = k.sb("h0num", [128, SEQ], F32)
            den = k.sb("h0den", [128, SEQ], F32)
            t_num = Tl()
            t_den = Tl()
            DILS = (1, 4, 16)
            seqi = 0
            combos = [(ml, g) for ml in range(2) for g in range(3)]

            def load_dil(ci):
                ml, g = combos[ci]
                d = DILS[g]
                gh = 2 * g + ml
                slot = ci % 2
                k.dma(k.sp, qT[slot][:], dr["qk"][8 + 2 * gh], reads=[t_qk[8 + 2 * gh]], writes=[t_q[slot]])
                k.dma(k.sp, kT[slot][:], dr["qk"][8 + 2 * gh + 1], reads=[t_qk[8 + 2 * gh + 1]], writes=[t_k[slot]])
                nb = 32 // d
                for r in range(d):
                    src = dr["vb"][g][:, ml * 128:(ml + 1) * 128].rearrange("(kb i dd) c -> dd i kb c", dd=d, i=128)[r]
                    k.dma(k.sp, vv[slot][:, r * nb:(r + 1) * nb, :], src, reads=[t_vb], writes=[t_v[slot]])
            load_dil(0)
            for ci, (ml, g) in enumerate(combos):
                s = ci % 2
                if ci + 1 < len(combos):
                    load_dil(ci + 1)
                d = DILS[g]
                nb = 32 // d
                scale = 128 ** -0.5
                units = [(r, u) for r in range(d) for u in range(nb // 2)]
                pendu = None
                for ui in range(len(units) + 1):
                    curu = None
                    if ui < len(units):
                        r, u = units[ui]

                        def qsl(b, r=r):
                            st0 = r + d * 128 * b
                            return slice(st0, st0 + d * 127 + 1, d)
                        sp_i = seqi % 3
                        seqi += 1
                        sp, tsp = H.P[sp_i], H.t_P[sp_i]
                        kbs = [max(2 * u - 1, 0), 2 * u, 2 * u, 2 * u + 1]
                        qbs = [2 * u, 2 * u, 2 * u + 1, 2 * u + 1]
                        for qi in range(4):
                            k.op(k.pe, lambda h, qi=qi: h.matmul(sp[:, qi * 128:(qi + 1) * 128], kT[s][:, qsl(kbs[qi])], qT[s][:, qsl(qbs[qi])],
                                                                 start=True, stop=True), reads=[t_k[s], t_q[s]], writes=[tsp])
                        curu = (r, u, sp, tsp, kbs, seqi)
                    if pendu is not None:
                        r, u, sp, tsp, kbs, sq_ = pendu
                        b = A.i % A.n
                        A.i += 1
                        pT, tpT = A.pT[b], A.t_pT[b]
                        k.op(k.act, lambda h: h.activation(out=pT[:], in_=sp[:], func=AF.Exp, scale=scale), reads=[tsp], writes=[tpT])
                        mk = H.mask4f if u == 0 else H.mask4
                        k.op(k.pool, lambda h: h.tensor_tensor(out=pT[:], in0=pT[:], in1=mk, op=ALU.mult), reads=[tpT, H.t], writes=[tpT])
                        po, tpo = H.P[3 + (sq_ % 2)], H.t_P[3 + (sq_ % 2)]
                        pd, tpd = H.P[5 + (sq_ % 2)], H.t_P[5 + (sq_ % 2)]
                        for qi in range(4):
                            oc = slice((qi // 2) * 128, (qi // 2 + 1) * 128)
                            k.op(k.pe, lambda h, qi=qi, oc=oc: h.matmul(po[:, oc], vv[s][:, r * nb + kbs[qi], :], pT[:, qi * 128:(qi + 1) * 128],
                                                                        start=(qi % 2 == 0), stop=(qi % 2 == 1)), reads=[tpT, t_v[s]], writes=[tpo])
                        for qi in range(4):
                            oc = slice((qi // 2) * 128, (qi // 2 + 1) * 128)
                            k.op(k.pe, lambda h, qi=qi, oc=oc: h.matmul(pd[:, oc], H.ones, pT[:, qi * 128:(qi + 1) * 128],
                                                                        start=(qi % 2 == 0), stop=(qi % 2 == 1)), reads=[tpT, H.t], writes=[tpd])
                        st0 = r + d * 256 * u
                        qcols = slice(st0, st0 + d * 255 + 1, d)
                        if g == 0:
                            k.op(k.dve, lambda h: h.tensor_copy(out=num[:, qcols], in_=po[:, 0:256]), reads=[tpo], writes=[t_num])
                            k.op(k.dve, lambda h: h.tensor_copy(out=den[:, qcols], in_=pd[:, 0:256]), reads=[tpd], writes=[t_den])
                        else:
                            k.op(k.dve, lambda h: h.tensor_tensor(out=num[:, qcols], in0=po[:, 0:256], in1=num[:, qcols], op=ALU.add),
                                 reads=[tpo, t_num], writes=[t_num])
                            k.op(k.dve, lambda h: h.tensor_tensor(out=den[:, qcols], in0=pd[:, 0:256], in1=den[:, qcols], op=ALU.add),
                                 reads=[tpd, t_den], writes=[t_den])
                    pendu = curu
                if g == 2:
                    for jj in range(8):
                        cs = slice(jj * 512, (jj + 1) * 512)
                        k.op(k.dve, lambda h: h.reciprocal(out=rc[0][:], in_=den[:, cs]), reads=[t_den], writes=[t_rc[0]])
                        o_ = ost[io % 2]
                        to_ = t_ost[io % 2]
                        io += 1
                        k.op(k.dve, lambda h: h.tensor_tensor(out=o_[:], in0=num[:, cs], in1=rc[0][:], op=ALU.mult),
                             reads=[t_num, t_rc[0]], writes=[to_])
                        k.dma(k.sp, odst(4 + ml, jj), o_[:], reads=[to_], writes=[t_oT], semt=t_oT)
            k.wait_all(k.sp, [t_oT])
            k.barrier()
    k.es = None


def h0_cols(r):
    cols = []
    for al in range(4):
        a = 4 * r + al
        cols += list(range(128 * a, 128 * a + 128))
        cols += list(range(1024 + 128 * a, 1024 + 128 * a + 128))
    for g in range(3):
        for ml in range(2):
            hi = 4 * g + 2 * r + ml
            cols += list(range(3072 + 128 * hi, 3072 + 128 * hi + 128))
            cols += list(range(4608 + 128 * hi, 4608 + 128 * hi + 128))
    for al in range(4):
        a = 4 * r + al
        cols += list(range(2048 + 128 * a, 2048 + 128 * a + 128))
    for g in range(3):
        for ml in range(2):
            hi = 4 * g + 2 * r + ml
            cols += list(range(6144 + 128 * hi, 6144 + 128 * hi + 128))
    return np.array(cols)


def h1_phase(k, dr, gather=None, opfx="o1m"):
    t_qn = [Tl() for _ in range(4)]
    t_qr = [Tl() for _ in range(4)]
    t_kn = [Tl() for _ in range(4)]
    t_kr = Tl()
    t_vc = Tl()
    t_qkd = [Tl() for _ in range(8)]
    t_vd = Tl()
    t_oT = Tl("oT")
    def hload(dst, tdst, tt):
        rk, i4 = tt // 4, tt % 4
        src = dr["hg%d" % i4][rk * D:(rk + 1) * D, :].rearrange("(c p) t -> p c t", p=128)
        k.dma(k.sp, dst[:], src, reads=[dr["_thg"][i4]], writes=[tdst])

    def odst(lc, j):
        return dr[opfx + "%d" % (lc // 2)][(lc % 2) * 128:(lc % 2 + 1) * 128, j * 512:(j + 1) * 512]
    with ExitStack() as es:
        k.es = es
        H = HConst(k, dr["consts"])
        with ExitStack() as es2:
            k.es = es2
            W = k.sb("h1w", [128, NCH, 2368], BF16)
            Wq = k.sb("h1wq", [128, 4, 768], BF16)
            Wkv = k.sb("h1wkv", [128, 2, 1024], BF16)
            ng = k.sb("h1ng", [128, 6], F32)
            t_W = Tl("h1w")
            WT = WTiles(k, W, dr["w_in"], 2368, 592)
            k.dma(k.pool, Wq[:], dr["w_uq"], writes=[t_W])
            k.dma(k.pool, Wkv[:], dr["w_ukv"], writes=[t_W])
            k.dma(k.pool, ng[:], dr["ng"], writes=[t_W])
            if gather is not None:
                gather()
            hs = [k.sb("h1h%d" % i, [128, NCH, 512], BF16) for i in range(2)]
            t_hs = [Tl(), Tl()]
            tab = [k.sb("h1tab%d" % i, [128, 4, 512], F32) for i in range(2)]
            t_tab = [Tl(), Tl()]
            vst = [k.sb("h1vst%d" % i, [128, 512], BF16) for i in range(2)]
            t_vst = [Tl(), Tl()]
            cl = k.sb("h1cl", [128, 6, 512], F32)
            sq = k.sb("h1sq", [128, 6, 512], BF16)
            cn = k.sb("h1cn", [128, 6, 512], BF16)
            rs = k.sb("h1rs", [128, 2, 512], F32)
            t_cl, t_sq, t_cn, t_rs = Tl(), Tl(), Tl(), Tl()
            R = RopeUnit(k, H)
            ivs = 0
            ip = 0
            for tt in range(8):
                i = tt % 2
                cs = slice(tt * 512, (tt + 1) * 512)
                hload(hs[i], t_hs[i], tt)
                k.dma(k.sp, tab[i][:, 0:2, :], dr["ropeA"].rearrange("a p t -> p a t")[:, :, cs], writes=[t_tab[i]])
                k.dma(k.sp, tab[i][:, 2:4, :], dr["ropeB"].rearrange("a p t -> p a t")[:, :, cs], writes=[t_tab[i]])

                def proj(c0, M, lhs=None, rhs=None, nk=NCH, rd=None):
                    nonlocal ip
                    ps, tps = H.P[ip % 2], H.t_P[ip % 2]
                    ip += 1
                    for kc in range(nk):
                        if lhs is None:
                            l_, r_, rd_ = W[:, kc, c0:c0 + M], hs[i][:, kc, :], WT.t(c0, M) + [t_hs[i]]
                        else:
                            l_, r_, rd_ = lhs[:, kc, c0:c0 + M], rhs[:, kc, :], rd
                        k.op(k.pe, lambda h, l_=l_, r_=r_, kc=kc: h.matmul(ps[0:M, :], l_, r_, start=(kc == 0), stop=(kc == nk - 1)),
                             reads=rd_, writes=[tps])
                    return ps, tps
                for c in range(6):
                    ps, tps = proj(c * 128, 128)
                    k.op(k.act, lambda h: h.activation(out=cl[:, c, :], in_=ps[:], func=AF.Copy), reads=[tps], writes=[t_cl])
                    k.op(k.act, lambda h: h.activation(out=sq[:, c, :], in_=cl[:, c, :], func=AF.Square), reads=[t_cl], writes=[t_sq])
                for (c0, nc_, which) in ((0, 4, 0), (4, 2, 1)):
                    pss, tpss = H.P[4], H.t_P[4]
                    for c in range(nc_):
                        k.op(k.pe, lambda h, c=c: h.matmul(pss[:], H.ones, sq[:, c0 + c, :], start=(c == 0), stop=(c == nc_ - 1)),
                             reads=[t_sq, H.t], writes=[tpss])
                    k.op(k.act, lambda h: h.activation(out=rs[:, which, :], in_=pss[:], func=AF.Sqrt, scale=1.0 / (nc_ * 128), bias=EPS),
                         reads=[tpss], writes=[t_rs])
                    k.op(k.dve, lambda h: h.reciprocal(out=rs[:, which, :], in_=rs[:, which, :]), reads=[t_rs], writes=[t_rs])
                    for c in range(nc_):
                        k.op(k.dve, lambda h, c=c: h.scalar_tensor_tensor(out=cn[:, c0 + c, :], in0=cl[:, c0 + c, :], scalar=ng[:, c0 + c:c0 + c + 1],
                                                                       in1=rs[:, which, :], op0=ALU.mult, op1=ALU.mult),
                             reads=[t_cl, t_rs, t_W], writes=[t_cn])
                ps, tps = proj(768, 64)
                R.store(ps, tps, 64, dr["kr"][:, cs], t_kr, H.permA, tab[i][0:64, 0, :], tab[i][0:64, 1, :], t_tab[i])
                for hl in range(4):
                    ps, tps = proj(hl * 192, 128, Wq, cn[:, 0:4, :], 4, [t_W, t_cn])
                    R.store(ps, tps, 128, dr["qn"][hl][:, cs], t_qn[hl])
                    ps, tps = proj(hl * 192 + 128, 64, Wq, cn[:, 0:4, :], 4, [t_W, t_cn])
                    R.store(ps, tps, 64, dr["qr"][hl][:, cs], t_qr[hl], H.permA, tab[i][0:64, 0, :], tab[i][0:64, 1, :], t_tab[i])
                    ps, tps = proj(hl * 128, 128, Wkv, cn[:, 4:6, :], 2, [t_W, t_cn])
                    R.store(ps, tps, 128, dr["kn"][hl][:, cs], t_kn[hl])
                for gi in range(8):
                    ps, tps = proj(832 + gi * 128, 128)
                    R.store(ps, tps, 128, dr["qkd"][gi][:, cs], t_qkd[gi], H.permB, tab[i][:, 2, :], tab[i][:, 3, :], t_tab[i])
                for tb in range(4):
                    rows = slice(tt * 512 + tb * 128, tt * 512 + (tb + 1) * 128)
                    for which in range(2):
                        ps, tps = H.P[2 + ivs % 2], H.t_P[2 + ivs % 2]
                        if which == 0:
                            for kc in range(2):
                                k.op(k.pe, lambda h, kc=kc: h.matmul(ps[:], cn[:, 4 + kc, tb * 128:(tb + 1) * 128], Wkv[:, kc, 512:1024],
                                                                     start=(kc == 0), stop=(kc == 1)), reads=[t_W, t_cn], writes=[tps])
                        else:
                            for kc in range(NCH):
                                k.op(k.pe, lambda h, kc=kc: h.matmul(ps[:], hs[i][:, kc, tb * 128:(tb + 1) * 128], W[:, kc, 1856:2368],
                                                                     start=(kc == 0), stop=(kc == NCH - 1)), reads=WT.t(1856, 512) + [t_hs[i]], writes=[tps])
                        v, tv = vst[ivs % 2], t_vst[ivs % 2]
                        ivs += 1
                        k.op(k.dve, lambda h: h.tensor_copy(out=v[:], in_=ps[:]), reads=[tps], writes=[tv])
                        dst, tdst = (dr["vc"], t_vc) if which == 0 else (dr["vd"], t_vd)
                        k.dma(k.sp, dst[rows, :], v[:], reads=[tv], writes=[tdst], semt=tdst)
            R.flush()
            k.barrier()
        with ExitStack() as es2:
            k.es = es2
            A = AttnBufs(k, 6)
            qT = [k.sb("h1q%d" % i, [128, SEQ], BF16) for i in range(2)]
            kT = [k.sb("h1k%d" % i, [128, SEQ], BF16) for i in range(2)]
            qr = [k.sb("h1qr%d" % i, [128, SEQ], BF16) for i in range(2)]
            kr = k.sb("h1kr", [128, SEQ], BF16)
            vv = [k.sb("h1v%d" % i, [128, 32, 128], BF16) for i in range(2)]
            t_q, t_k, t_v, t_qrs = [Tl(), Tl()], [Tl(), Tl()], [Tl(), Tl()], [Tl(), Tl()]
            t_krs = Tl()
            rc = k.sb("h1rc", [128, 512], F32)
            t_rc = Tl()
            ost = [k.sb("h1ost%d" % i, [128, 512], BF16) for i in range(2)]
            t_ost = [Tl(), Tl()]
            io = 0
            k.dma(k.sp, kr[0:64, :], dr["kr"], reads=[t_kr], writes=[t_krs])
            k.dma(k.sp, kr[64:128, :], dr["kr"], reads=[t_kr], writes=[t_krs])
            vc_v = dr["vc"].rearrange("(b p) c -> p b c", p=128)
            vd_v = dr["vd"].rearrange("(b p) c -> p b c", p=128)

            def load_c(hl):
                s = hl % 2
                k.dma(k.sp, qT[s][:], dr["qn"][hl], reads=[t_qn[hl]], writes=[t_q[s]])
                k.dma(k.sp, qr[s][0:64, :], dr["qr"][hl], reads=[t_qr[hl]], writes=[t_qrs[s]])
                k.dma(k.sp, qr[s][64:128, :], dr["qr"][hl], reads=[t_qr[hl]], writes=[t_qrs[s]])
                k.dma(k.sp, kT[s][:], dr["kn"][hl], reads=[t_kn[hl]], writes=[t_k[s]])
                k.dma(k.sp, vv[s][:], vc_v[:, :, hl * 128:(hl + 1) * 128], reads=[t_vc], writes=[t_v[s]])

            def load_d(dl):
                s = dl % 2
                k.dma(k.sp, qT[s][:], dr["qkd"][2 * dl], reads=[t_qkd[2 * dl]], writes=[t_q[s]])
                k.dma(k.sp, kT[s][:], dr["qkd"][2 * dl + 1], reads=[t_qkd[2 * dl + 1]], writes=[t_k[s]])
                k.dma(k.sp, vv[s][:], vd_v[:, :, dl * 128:(dl + 1) * 128], reads=[t_vd], writes=[t_v[s]])

            def finish(j, row0):
                nonlocal io
                Oi, Di = 3 + 2 * (j % 2), 4 + 2 * (j % 2)
                k.op(k.dve, lambda h: h.reciprocal(out=rc[:], in_=H.P[Di][:]), reads=[H.t_P[Di]], writes=[t_rc])
                o_, to_ = ost[io % 2], t_ost[io % 2]
                io += 1
                k.op(k.dve, lambda h: h.tensor_tensor(out=o_[:], in0=H.P[Oi][:], in1=rc[:], op=ALU.mult), reads=[H.t_P[Oi], t_rc], writes=[to_])
                k.dma(k.sp, odst(row0 // 128, j), o_[:], reads=[to_], writes=[t_oT], semt=t_oT)
            load_c(0)
            for hl in range(4):
                s = hl % 2
                if hl + 1 < 4:
                    load_c(hl + 1)
                for j in range(8):
                    st = dict(parts=lambda kb, q_lo: [(kT[s][:, kb * 128:(kb + 1) * 128], qT[s][:, j * 512 + q_lo:(j + 1) * 512], [t_k[s], t_q[s]]),
                                                     (kr[64 * (kb % 2):64 * (kb % 2) + 64, kb * 128:(kb + 1) * 128],
                                                      qr[s][64 * (kb % 2):64 * (kb % 2) + 64, j * 512 + q_lo:(j + 1) * 512], [t_krs, t_qrs[s]])],
                              v=lambda kb: (vv[s][:, kb, :], [t_v[s]]), scale=192 ** -0.5, O=3 + 2 * (j % 2), D=4 + 2 * (j % 2))
                    run_causal_tile(k, H, A, j, [st], [0, 1, 2, 7])
                    finish(j, hl * 128)
            esel = k.sb("h1esel", [48, 2048], BF16)
            t_esel = Tl()
            k.dma(k.pool, esel[:], dr["esel"], writes=[t_esel])
            km32 = k.sb("h1km32", [128, 16], F32)
            km = k.sb("h1km", [128, 16], BF16)
            gm = k.sb("h1gm", [128, 32, 16], F32)
            mx = k.sb("h1mx", [128, 32, 8], F32)
            nm = k.sb("h1nm", [128, 32, 48], BF16)
            t_gmq = [Tl() for _ in range(32)]
            t_mxq = [Tl() for _ in range(32)]
            t_nmq = [Tl() for _ in range(32)]
            t_pgq = [Tl() for _ in range(32)]
            t_pnq = [Tl() for _ in range(8)]
            negT = k.sb("h1negT", [48, SEQ], BF16)
            t_km, t_gm, t_mx, t_nm, t_neg = Tl(), Tl(), Tl(), Tl(), Tl()
            load_d(0)
            for dl in range(4):
                s = dl % 2
                if dl + 1 < 4:
                    load_d(dl + 1)
                k.op(k.dve, lambda h: h.reduce_sum(out=km32[:], in_=kT[s][:].rearrange("p (n l) -> p n l", l=256), axis=mybir.AxisListType.X),
                     reads=[t_k[s]], writes=[t_km])
                k.op(k.dve, lambda h: h.tensor_scalar(out=km[:], in0=km32[:], scalar1=1.0 / 256, scalar2=None, op0=ALU.mult), reads=[t_km], writes=[t_km])
                k.op(k.pool, lambda h: h.memset(gm[:], -1e30), writes=t_gmq)
                k.op(k.pool, lambda h: h.memset(nm[:], 0.0), writes=t_nmq)
                k.op(k.pool, lambda h: h.memset(negT[:], 0.0), writes=[t_neg])
                pg, tpg = H.P[7], H.t_P[7]
                for qb in range(2, 32):
                    k.op(k.pe, lambda h: h.matmul(pg[:, qb * 16:(qb + 1) * 16], qT[s][:, qb * 128:(qb + 1) * 128], km[:], start=True, stop=True),
                         reads=[t_q[s], t_km], writes=[tpg])
                for qb in range(2, 32):
                    own = qb // 2
                    k.op(k.dve, lambda h: h.tensor_copy(out=gm[:, qb, 0:own], in_=pg[:, qb * 16:qb * 16 + own]), reads=[tpg], writes=[t_gmq[qb]])
                    k.op(k.dve, lambda h: h.max(out=mx[:, qb, :], in_=gm[:, qb, :]), reads=[t_gmq[qb]], writes=[t_mxq[qb]])
                    k.op(k.dve, lambda h: h.tensor_scalar(out=nm[:, qb, 0:own], in0=gm[:, qb, 0:own], scalar1=mx[:, qb, 2:3], scalar2=NEG, op0=ALU.is_lt, op1=ALU.mult),
                         reads=[t_gmq[qb], t_mxq[qb]], writes=[t_nmq[qb]])
                    k.op(k.dve, lambda h: h.tensor_copy(out=nm[:, qb, 32:32 + own], in_=nm[:, qb, 0:own]), reads=[t_nmq[qb]], writes=[t_nmq[qb]])
                for g4 in range(8):
                    bank = 5 + g4 % 2
                    pn, tpn = H.P[bank], H.t_P[bank]
                    q0 = 2 if g4 == 0 else 4 * g4
                    for qb in range(q0, 4 * g4 + 4):
                        k.op(k.pe, lambda h: h.matmul(pn[0:48, (qb % 4) * 128:(qb % 4 + 1) * 128], nm[:, qb, :], H.ident, start=True, stop=True),
                             reads=[t_nmq[qb], H.t], writes=[tpn])
                    k.op(k.act, lambda h: h.activation(out=negT[:, q0 * 128:(4 * g4 + 4) * 128], in_=pn[0:48, (q0 % 4) * 128:512], func=AF.Copy),
                         reads=[tpn], writes=[t_neg])
                for j in range(8):
                    def parts(kb, q_lo):
                        p = [(kT[s][:, kb * 128:(kb + 1) * 128], qT[s][:, j * 512 + q_lo:(j + 1) * 512], [t_k[s], t_q[s]])]
                        if kb <= 4 * j + 1:
                            n = kb // 2
                            r0 = 32 * (kb % 2)
                            p.append((esel[r0:r0 + 16, n * 128:(n + 1) * 128], negT[r0:r0 + 16, j * 512 + q_lo:(j + 1) * 512], [t_esel, t_neg]))
                        return p
                    st = dict(parts=parts, v=lambda kb: (vv[s][:, kb, :], [t_v[s]]), scale=128 ** -0.5, O=3 + 2 * (j % 2), D=4 + 2 * (j % 2))
                    run_causal_tile(k, H, A, j, [st], [0, 1, 2, 7])
                    finish(j, 512 + dl * 128)
            k.wait_all(k.sp, [t_oT])
            k.barrier()
    k.es = None


def make_esel():
    e = np.zeros((48, 2048), np.float32)
    for n in range(16):
        e[n, n * 128:(n + 1) * 128] = 1.0
        e[32 + n, n * 128:(n + 1) * 128] = 1.0
    return e


def h1_layout(r, cd_w_in, cd_w_uq, cd_w_ukv, q_norm, kv_norm):
    cols = list(range(0, 832))
    for dl in range(4):
        hd = 4 * r + dl
        cols += list(range(832 + 128 * hd, 832 + 128 * hd + 128))
        cols += list(range(832 + 1024 + 128 * hd, 832 + 1024 + 128 * hd + 128))
    for dl in range(4):
        hd = 4 * r + dl
        cols += list(range(832 + 2048 + 128 * hd, 832 + 2048 + 128 * hd + 128))
    w_in = lay_rows(cd_w_in[:, np.array(cols)])
    uq_cols = []
    for hl in range(4):
        hc = 4 * r + hl
        uq_cols += list(range(192 * hc, 192 * hc + 192))
    w_uq = lay_rows(cd_w_uq[:, np.array(uq_cols)])
    kv_cols = []
    for hl in range(4):
        hc = 4 * r + hl
        kv_cols += list(range(256 * hc, 256 * hc + 128))
    for hl in range(4):
        hc = 4 * r + hl
        kv_cols += list(range(256 * hc + 128, 256 * hc + 256))
    w_ukv = lay_rows(cd_w_ukv[:, np.array(kv_cols)])
    ng = np.concatenate([q_norm.reshape(4, 128).T, kv_norm.reshape(2, 128).T], axis=1).astype(np.float32)
    return w_in, w_uq, w_ukv, np.ascontiguousarray(ng)


G_FFN = {(0, 0): 0, (0, 1): 1, (1, 0): 2, (1, 1): 3}
G_MIX = {0: 4, 1: 5}
G_PLE = {0: 6, 1: 7}
G_FINAL = 8
NTOK = 2048


def build_fused():
    nc = bass.Bass("TRN2", target_bir_lowering=False)
    dr = {}

    def dt(name, shape, dty=F32, kind="ExternalInput"):
        dr[name] = nc.dram_tensor(name, list(shape), dty, kind=kind).ap()
    dt("xT", [D, NTOK])
    dt("gains", [128, 9 * NCH])
    dt("sel", [128, 12])
    for l in range(2):
        for i in range(2):
            dt("wgu%d%d" % (l, i), [NHC, 128, NCH, 256])
            dt("wd%d%d" % (l, i), [2, NCH, 128, 22, 128])
        dt("wout%d" % l, [NCH, 128, 12 if l == 0 else 16, 128])
        dt("wgate%d" % l, [NCH, 128, NCH, 128])
        dt("wproj%d" % l, [128, 2, D])
        dt("pT%d" % l, [256, NTOK])
    dt("w_in0", [128, NCH, 3840])
    dt("ropeA", [2, 128, SEQ])
    dt("ropeB", [2, 128, SEQ])
    dt("consts", [128, 1792])
    dt("lamb", [128, 256])
    dt("subln", [128, 1])
    dt("w_in1", [128, NCH, 2368])
    dt("w_uq", [128, 4, 768])
    dt("w_ukv", [128, 2, 1024])
    dt("ng", [128, 6])
    dt("esel", [48, 2048])
    dt("outT", [D, NTOK], F32, "ExternalOutput")
    I = "Internal"
    dt("x1", [D, NTOK], F32, I)
    dt("x2", [D, NTOK], F32, I)
    for i4 in range(4):
        dt("hm%d" % i4, [D, 512], BF16, I)
        dt("hg%d" % i4, [2 * D, 512], BF16, I)
        dt("o1m%d" % i4, [256, SEQ], BF16, I)
        dt("o1g%d" % i4, [512, SEQ], BF16, I)
    for i3 in range(3):
        dt("o0m%d" % i3, [256, SEQ], BF16, I)
        dt("o0g%d" % i3, [512, SEQ], BF16, I)
    dt("o0s", [1536, NTOK], BF16, I)
    dt("o1s", [2048, NTOK], BF16, I)
    dt("qk", [20, 128, SEQ], BF16, I)
    dt("va", [SEQ, 512], BF16, I)
    dt("vb", [3, SEQ, 256], BF16, I)
    for nm_, shp in (("qn", [4, 128, SEQ]), ("qr", [4, 64, SEQ]), ("kn", [4, 128, SEQ]), ("kr", [64, SEQ]), ("vc", [SEQ, 512]),
                     ("qkd", [8, 128, SEQ]), ("vd", [SEQ, 512])):
        dt(nm_, shp, BF16, I)

    def sub(**kw):
        d = dict(dr)
        d.update({a: dr[b] for a, b in kw.items()})
        return d

    with ExitStack() as es:
        k = K(nc, es)
        k.es = es
        sel = k.sb("sel", [128, 12], F32)
        t_sel = Tl("sel")
        k.dma(k.sp, sel[:], dr["sel"], writes=[t_sel])

        def tphase(prefix, steps, d, pre=None):
            with ExitStack() as es2:
                k.es = es2
                k.prefix = prefix
                S = TState(k)
                d = dict(d)
                d["_sel"] = (sel, t_sel)
                t_phase(k, S, NTOK, d, steps, pre=pre)
                k.barrier()

        t_hg = [Tl("hg%d" % i) for i in range(4)]
        dr["_thg"] = t_hg

        def hgather():
            k.pair_gather([dr["hm%d" % i][:, :] for i in range(4)], [dr["hg%d" % i][:, :] for i in range(4)], tiles=t_hg)

        tphase("t1_", [("ffn", "wgu00", "wd00", G_FFN[(0, 0)] * NCH), ("hout", G_MIX[0] * NCH)],
               sub(xT="xT", xT_out="x1"))
        k.prefix = "h0_"
        h0_phase(k, sub(w_in="w_in0"), gather=hgather, opfx="o0m")
        k.barrier()
        tphase("t2_", [("oproj", "wout0", 12, "o0g", 768), ("ffn", "wgu01", "wd01", G_FFN[(0, 1)] * NCH), ("ple", "wgate0", "wproj0", "pT0", G_PLE[0] * NCH),
                       ("ffn", "wgu10", "wd10", G_FFN[(1, 0)] * NCH), ("hout", G_MIX[1] * NCH)],
               sub(xT="x1", xT_out="x2"),
               pre=lambda: k.pair_gather([dr["o0m%d" % i][:, :] for i in range(3)], [dr["o0g%d" % i][:, :] for i in range(3)]))
        k.prefix = "h1_"
        h1_phase(k, sub(w_in="w_in1"), gather=hgather, opfx="o1m")
        k.barrier()
        tphase("t3_", [("oproj", "wout1", 16, "o1g", 1024), ("ffn", "wgu11", "wd11", G_FFN[(1, 1)] * NCH), ("ple", "wgate1", "wproj1", "pT1", G_PLE[1] * NCH),
                       ("final", G_FINAL * NCH)],
               sub(xT="x2", xT_out="outT"),
               pre=lambda: k.pair_gather([dr["o1m%d" % i][:, :] for i in range(4)], [dr["o1g%d" % i][:, :] for i in range(4)]))
    return nc


def kernel(x, p, ffn_norm, ffn_w_gu, ffn_w_down, mix_norm, ab_w_in, ab_lambda, ab_subln, ab_w_out,
           cd_w_in, cd_q_norm, cd_w_uq, cd_kv_norm, cd_w_ukv, cd_w_out, ple_norm, ple_w_gate,
           ple_w_proj, final_norm):
    f = lambda a: np.asarray(a, dtype=np.float32)
    x, p = f(x), f(p)
    ffn_norm, ffn_w_gu, ffn_w_down, mix_norm = f(ffn_norm), f(ffn_w_gu), f(ffn_w_down), f(mix_norm)
    ab_w_in, ab_lambda, ab_subln, ab_w_out = f(ab_w_in), f(ab_lambda), f(ab_subln), f(ab_w_out)
    cd_w_in, cd_q_norm, cd_w_uq, cd_kv_norm, cd_w_ukv, cd_w_out = f(cd_w_in), f(cd_q_norm), f(cd_w_uq), f(cd_kv_norm), f(cd_w_ukv), f(cd_w_out)
    ple_norm, ple_w_gate, ple_w_proj, final_norm = f(ple_norm), f(ple_w_gate), f(ple_w_proj), f(final_norm)

    shared = {
        "gains": lay_gains([ffn_norm[0, 0], ffn_norm[0, 1], ffn_norm[1, 0], ffn_norm[1, 1], mix_norm[0], mix_norm[1],
                            ple_norm[0], ple_norm[1], final_norm]),
        "consts": make_consts(), "ropeA": make_rope(64, 128, 32), "ropeB": make_rope(128, 128, 64), "esel": make_esel(),
        "lamb": np.ascontiguousarray(np.broadcast_to(ab_lambda[0].reshape(1, 256), (128, 256))),
        "subln": np.ascontiguousarray(ab_subln[0].reshape(128, 1)),
        "wout0": lay_wsq(ab_w_out[0]), "wout1": lay_wsq(cd_w_out[0]),
    }
    for l in range(2):
        for i in range(2):
            shared["wgu%d%d" % (l, i)] = lay_wgu(ffn_w_gu[l, i])
            shared["wd%d%d" % (l, i)] = lay_wd(ffn_w_down[l, i])
        shared["wgate%d" % l] = lay_wsq(ple_w_gate[l])
        shared["wproj%d" % l] = lay_rows(ple_w_proj[l])
    w0 = [lay_rows(ab_w_in[0][:, h0_cols(r)]) for r in range(2)]
    lay1 = [h1_layout(r, cd_w_in[0], cd_w_uq[0], cd_w_ukv[0], cd_q_norm[0], cd_kv_norm[0]) for r in range(2)]
    in_maps = []
    for c in range(8):
        b, r = c // 2, c % 2
        ts = slice(r * NTOK, (r + 1) * NTOK)
        m = dict(shared)
        m["xT"] = np.ascontiguousarray(x[b, ts, :].T)
        for l in range(2):
            m["pT%d" % l] = np.ascontiguousarray(p[l, b, ts, :].T)
        sel = np.zeros((128, 12), np.float32)
        sel[:, r] = 1.0
        m["sel"] = sel
        m["w_in0"] = w0[r]
        m["w_in1"], m["w_uq"], m["w_ukv"], m["ng"] = lay1[r]
        in_maps.append(m)
    nc = build_fused()
    res = run_bass_kernel_spmd(nc, in_maps, core_ids=list(range(8))).results
    out = np.empty((4, SEQ, D), np.float32)
    for c in range(8):
        b, r = c // 2, c % 2
        out[b, r * NTOK:(r + 1) * NTOK, :] = res[c]["outT"].T
    return out
```
